# Optimizing a Trainium2 kernel written in Bass

```python
import math
import jax, jax.numpy as jnp
from jax import lax
import numpy as np

D_MODEL = 2048
BATCH = 8
SEQ = 2048
DEPTH = 2

GRID_W = 64
CTX_LEN = 256
MIX_WIDTH = D_MODEL
A_HEAD_DIM = 128
A_WIDTH = D_MODEL // 2
A_HEADS = A_WIDTH // A_HEAD_DIM
B_HEADS = 4
B_VAL_WIDTH = D_MODEL // 2
B_VAL_DIM = B_VAL_WIDTH // B_HEADS
B_KEY_WIDTH = B_VAL_WIDTH // 2
B_KEY_DIM = B_KEY_WIDTH // B_HEADS
GATE_RANK = 16
GLA_GATE_TEMP = 16.0
C_HEAD_DIM = 128
C_HEADS = D_MODEL // (2 * C_HEAD_DIM)
C_VAL_DIM = 2 * C_HEAD_DIM
Q_BLOCK = 128
CHUNK = 32
ROPE_BASE = 10000.0
EPS = 1e-6
N_EVEN = (DEPTH + 1) // 2
N_ODD = DEPTH // 2
EVEN_IN_WIDTH = 4 * A_WIDTH + 2 * B_KEY_WIDTH + B_VAL_WIDTH + 2 * GATE_RANK + MIX_WIDTH
ODD_IN_WIDTH = 4 * MIX_WIDTH

kernel_name = "hybrid_hgrn2_gla_diffattn_dit"


def rms_norm(x, gain):
    xf = x.astype(jnp.float32)
    y = xf * lax.rsqrt(jnp.mean(xf * xf, axis=-1, keepdims=True) + EPS)
    return (y * gain.astype(jnp.float32)).astype(x.dtype)


def adaln(cond, w, b):
    m = jax.nn.silu(cond) @ w + b
    return jnp.split(m, 3, axis=-1)


def to_heads(z, n_heads):
    bsz, length, _ = z.shape
    return z.reshape(bsz, length, n_heads, -1).transpose(0, 2, 1, 3)


def from_heads(z):
    bsz, n_heads, length, d = z.shape
    return z.transpose(0, 2, 1, 3).reshape(bsz, length, n_heads * d)


def grid_positions(n_tokens):
    rows = n_tokens // GRID_W
    pos_r = jnp.repeat(jnp.arange(rows, dtype=jnp.int32), GRID_W)
    pos_c = jnp.tile(jnp.arange(GRID_W, dtype=jnp.int32), rows)
    return pos_r, pos_c


def rope_1d(x, pos):
    half = x.shape[-1] // 2
    inv = ROPE_BASE ** (-jnp.arange(half, dtype=jnp.float32) / half)
    ang = pos.astype(jnp.float32)[:, None] * inv
    cos, sin = jnp.cos(ang), jnp.sin(ang)
    x1, x2 = x[..., :half], x[..., half:]
    return jnp.concatenate([x1 * cos - x2 * sin, x2 * cos + x1 * sin], axis=-1)


def rope_2d(x, pos_r, pos_c):
    r = x.shape[-1] // 2
    return jnp.concatenate([rope_1d(x[..., :r], pos_r), rope_1d(x[..., r:], pos_c)], axis=-1)


def gated_linear_scan(q, k, v, log_a, s0):
    bsz, n_heads, length, dk = q.shape
    n_chunks = length // CHUNK

    def to_chunks(z):
        z = z.astype(jnp.float32).reshape(bsz, n_heads, n_chunks, CHUNK, z.shape[-1])
        return jnp.moveaxis(z, 2, 0)

    mask = jnp.tril(jnp.ones((CHUNK, CHUNK), dtype=bool))

    def step(s, xs):
        qn, kn, vn, an = xs
        b = jnp.cumsum(an, axis=-2)
        q_dec = qn * jnp.exp(b)
        k_inv = kn * jnp.exp(-b)
        scores = jnp.where(mask, jnp.einsum("bhtd,bhsd->bhts", q_dec, k_inv), 0.0)
        o = jnp.einsum("bhts,bhsv->bhtv", scores, vn) + jnp.einsum("bhtd,bhdv->bhtv", q_dec, s)
        b_last = b[..., -1:, :]
        k_tail = kn * jnp.exp(b_last - b)
        s_new = jnp.exp(b_last)[..., 0, :, None] * s + jnp.einsum("bhsd,bhsv->bhdv", k_tail, vn)
        return s_new, o

    s_fin, o_chunks = lax.scan(step, s0.astype(jnp.float32),
                               (to_chunks(q), to_chunks(k), to_chunks(v), to_chunks(log_a)))
    o = jnp.moveaxis(o_chunks, 0, 2).reshape(bsz, n_heads, length, v.shape[-1])
    return o, s_fin


def bidir_prefix_scan(q, k_f, k_b, v, la_f, la_b, n_ctx):
    bsz, n_heads, _, dk = q.shape
    s0 = jnp.zeros((bsz, n_heads, dk, v.shape[-1]), jnp.float32)

    def split(z):
        return z[:, :, :n_ctx], z[:, :, n_ctx:]

    def flip(z):
        return jnp.flip(z, axis=2)

    qc, ql = split(q)
    kfc, kfl = split(k_f)
    kbc, kbl = split(k_b)
    vc, vl = split(v)
    afc, afl = split(la_f)
    abc, abl = split(la_b)
    o_cf, s_f = gated_linear_scan(qc, kfc, vc, afc, s0)
    o_lf, _ = gated_linear_scan(ql, kfl, vl, afl, s_f)
    o_cb, s_b = gated_linear_scan(flip(qc), flip(kbc), flip(vc), flip(abc), s0)
    o_lb, _ = gated_linear_scan(flip(ql), flip(kbl), flip(vl), flip(abl), s_b)
    return jnp.concatenate([o_cf + flip(o_cb), o_lf + flip(o_lb)], axis=2)


def even_mixer(h, n_ctx, layer, lb_logits, w_in, w_a2, b_a2, a_gain, b_gain, w_out):
    sizes = [A_WIDTH] * 4 + [B_KEY_WIDTH, B_KEY_WIDTH, B_VAL_WIDTH, GATE_RANK, GATE_RANK]
    cuts = np.cumsum(sizes).tolist()
    a_q, a_ff, a_fb, a_i, b_q, b_k, b_v, b_zf, b_zb, gate = jnp.split(h @ w_in, cuts, axis=-1)
    lb = jnp.cumsum(jax.nn.softmax(lb_logits.astype(jnp.float32), axis=0), axis=0)[layer]
    f_f = lb + (1.0 - lb) * jax.nn.sigmoid(a_ff.astype(jnp.float32))
    f_b = lb + (1.0 - lb) * jax.nn.sigmoid(a_fb.astype(jnp.float32))
    o_a = bidir_prefix_scan(
        to_heads(jax.nn.silu(a_q), A_HEADS),
        to_heads(1.0 - f_f, A_HEADS), to_heads(1.0 - f_b, A_HEADS),
        to_heads(a_i, A_HEADS),
        to_heads(jnp.log(f_f), A_HEADS), to_heads(jnp.log(f_b), A_HEADS), n_ctx)
    o_a = from_heads(rms_norm(o_a, a_gain))
    la_f = jax.nn.log_sigmoid((b_zf @ w_a2[0] + b_a2[0]).astype(jnp.float32)) / GLA_GATE_TEMP
    la_b = jax.nn.log_sigmoid((b_zb @ w_a2[1] + b_a2[1]).astype(jnp.float32)) / GLA_GATE_TEMP
    k_heads = to_heads(b_k, B_HEADS)
    o_b = bidir_prefix_scan(
        to_heads(b_q * (B_KEY_DIM ** -0.5), B_HEADS), k_heads, k_heads,
        to_heads(b_v, B_HEADS),
        to_heads(la_f, B_HEADS), to_heads(la_b, B_HEADS), n_ctx)
    o_b = from_heads(rms_norm(o_b, b_gain))
    o = jnp.concatenate([o_a, o_b], axis=-1).astype(h.dtype) * jax.nn.silu(gate)
    return o @ w_out


def odd_mixer(h, n_ctx, layer, w_in, q_gain, k_gain, lam_qk, o_gain, w_out, with_ctx_out):
    bsz, length, _ = h.shape
    n_lat = length - n_ctx
    q, k, v, gate = jnp.split(h @ w_in, 4, axis=-1)

    def pair_heads(z):
        return z.reshape(bsz, length, C_HEADS, 2, C_HEAD_DIM).transpose(3, 0, 2, 1, 4)

    q = rms_norm(pair_heads(q), q_gain)
    k = rms_norm(pair_heads(k), k_gain)
    v = to_heads(v, C_HEADS)
    pos_r, pos_c = grid_positions(n_lat)
    q_lat = rope_2d(q[..., n_ctx:, :], pos_r, pos_c)
    k_all = jnp.concatenate([k[..., :n_ctx, :].astype(jnp.float32),
                             rope_2d(k[..., n_ctx:, :], pos_r, pos_c)], axis=-2)
    lam_init = 0.8 - 0.6 * math.exp(-0.3 * layer)
    lq = lam_qk.astype(jnp.float32)
    lam = jnp.exp(jnp.sum(lq[0] * lq[1])) - jnp.exp(jnp.sum(lq[2] * lq[3])) + lam_init
    scale = C_HEAD_DIM ** -0.5

    def diff_attend(qq, kk, vv):
        s = jnp.einsum("nbhqd,nbhkd->nbhqk", qq.astype(jnp.float32), kk.astype(jnp.float32)) * scale
        p = jax.nn.softmax(s, axis=-1)
        return jnp.einsum("bhqk,bhkv->bhqv", p[0] - lam * p[1], vv.astype(jnp.float32))

    n_blk = n_lat // Q_BLOCK
    q_blocks = jnp.moveaxis(q_lat.reshape(2, bsz, C_HEADS, n_blk, Q_BLOCK, C_HEAD_DIM), 3, 0)
    o_lat = lax.map(lambda qb: diff_attend(qb, k_all, v), q_blocks)
    o = jnp.moveaxis(o_lat, 0, 2).reshape(bsz, C_HEADS, n_lat, C_VAL_DIM)
    if with_ctx_out:
        o_ctx = diff_attend(q[..., :n_ctx, :], k[..., :n_ctx, :], v[:, :, :n_ctx])
        o = jnp.concatenate([o_ctx, o], axis=2)
    else:
        gate = gate[:, n_ctx:]
    o = rms_norm(o, o_gain) * (1.0 - lam_init)
    return (from_heads(o).astype(h.dtype) * jax.nn.silu(gate)) @ w_out


def setup_inputs(seed: int = 0) -> dict:
    key = jax.random.key(seed)
    ks = jax.random.split(key, 20)
    f32 = jnp.float32
    D = D_MODEL

    def nrm(k, shape, scale):
        return jax.random.normal(k, shape, f32) * scale

    return {
        "x": nrm(ks[0], (BATCH, SEQ, D), 1.0),
        "c": nrm(ks[1], (BATCH, D), 1.0),
        "ctx": nrm(ks[2], (BATCH, CTX_LEN, D), 1.0),
        "c_ctx": nrm(ks[3], (D,), 1.0),
        "norm_gain": 1.0 + nrm(ks[4], (DEPTH, D), 0.02),
        "w_ada": nrm(ks[5], (DEPTH, D, 3 * D), 0.5 * D ** -0.5),
        "b_ada": nrm(ks[6], (DEPTH, 3 * D), 0.02),
        "lb_logits": nrm(ks[7], (DEPTH + 1, A_WIDTH), 0.1),
        "w_in_even": nrm(ks[8], (N_EVEN, D, EVEN_IN_WIDTH), D ** -0.5),
        "w_a2": nrm(ks[9], (N_EVEN, 2, GATE_RANK, B_KEY_WIDTH), GATE_RANK ** -0.5),
        "b_a2": nrm(ks[10], (N_EVEN, 2, B_KEY_WIDTH), 0.1),
        "a_out_gain": 1.0 + nrm(ks[11], (N_EVEN, A_HEAD_DIM), 0.02),
        "b_out_gain": 1.0 + nrm(ks[12], (N_EVEN, B_VAL_DIM), 0.02),
        "w_out_even": nrm(ks[13], (N_EVEN, MIX_WIDTH, D), MIX_WIDTH ** -0.5),
        "w_in_odd": nrm(ks[14], (N_ODD, D, ODD_IN_WIDTH), D ** -0.5),
        "q_norm_gain": 1.0 + nrm(ks[15], (N_ODD, C_HEAD_DIM), 0.02),
        "k_norm_gain": 1.0 + nrm(ks[16], (N_ODD, C_HEAD_DIM), 0.02),
        "lambda_qk": nrm(ks[17], (N_ODD, 4, C_HEAD_DIM), 0.1),
        "c_out_gain": 1.0 + nrm(ks[18], (N_ODD, C_VAL_DIM), 0.02),
        "w_out_odd": nrm(ks[19], (N_ODD, MIX_WIDTH, D), MIX_WIDTH ** -0.5),
    }


def reference(x, c, ctx, c_ctx, norm_gain, w_ada, b_ada, lb_logits, w_in_even, w_a2, b_a2,
              a_out_gain, b_out_gain, w_out_even, w_in_odd, q_norm_gain, k_norm_gain,
              lambda_qk, c_out_gain, w_out_odd):
    n_ctx = ctx.shape[1]
    n_lat = x.shape[1]
    for l in range(DEPTH):
        last = l == DEPTH - 1
        j = l // 2
        shift, scale, gate = adaln(c, w_ada[l], b_ada[l])
        shift_c, scale_c, gate_c = adaln(c_ctx, w_ada[l], b_ada[l])
        h_lat = rms_norm(x, norm_gain[l]) * (1.0 + scale[:, None]) + shift[:, None]
        h_ctx = rms_norm(ctx, norm_gain[l]) * (1.0 + scale_c) + shift_c
        h = jnp.concatenate([h_ctx, h_lat.astype(h_ctx.dtype)], axis=1)
        if l % 2 == 0:
            out = even_mixer(h, n_ctx, l, lb_logits, w_in_even[j], w_a2[j], b_a2[j],
                             a_out_gain[j], b_out_gain[j], w_out_even[j])
        else:
            out = odd_mixer(h, n_ctx, l, w_in_odd[j], q_norm_gain[j], k_norm_gain[j],
                            lambda_qk[j], c_out_gain[j], w_out_odd[j], not last)
        x = x + gate[:, None] * out[:, -n_lat:]
        if not last:
            ctx = ctx + gate_c * out[:, :n_ctx]
    return x
```

```python
import math
import numpy as np
from contextlib import ExitStack
import concourse.bass as bass
import concourse.mybir as mybir
from concourse.bass_utils import run_bass_kernel_spmd

F32 = mybir.dt.float32
BF16 = mybir.dt.bfloat16
ALU = mybir.AluOpType
AF = mybir.ActivationFunctionType
AX = mybir.AxisListType

D = 2048
SEQ = 2048
NCTX = 256
L = SEQ + NCTX
NT = L // 128
KC = D // 128
EPS = 1e-6
LAM_INIT = 0.8 - 0.6 * math.exp(-0.3 * 1)
BLOCKS = [(0, 256), (256, 512), (768, 512), (1280, 512), (1792, 512)]


class Buf:
    __slots__ = ("name", "w", "r")

    def __init__(self, name):
        self.name = name
        self.w = None
        self.r = []


class Tile:
    def __init__(self, t, name):
        self.t = t
        self.b = Buf(name)

    def __getitem__(self, k):
        return self.t[k]


class Op:
    __slots__ = ("eng", "fn", "deps", "signal", "sem", "value", "dma", "idx")

    def __init__(self, eng, fn, dma):
        self.eng = eng
        self.fn = fn
        self.deps = set()
        self.signal = False
        self.sem = None
        self.value = 0
        self.dma = dma
        self.idx = 0


class _SemCtr:
    def __init__(self, prog, name, limit):
        self.prog = prog
        self.name = name
        self.limit = limit
        self.sem = None
        self.val = 0
        self.n = 0

    def bump(self, inc):
        if self.sem is None or self.val + inc > self.limit:
            self.sem = self.prog.new_sem(f"{self.name}_{self.n}")
            self.n += 1
            self.val = 0
        self.val += inc
        return self.sem, self.val


class Prog:
    ENGS = ("pe", "act", "dve", "pool", "sp")

    def __init__(self, nc, stack):
        self.nc = nc
        self.stack = stack
        self.ops = {e: [] for e in self.ENGS}
        self.nops = 0
        self.nsem = 0
        self.pending_dma = []
        self.last_real = {e: None for e in self.ENGS}

    def init_arena(self, words):
        self.arena = self.stack.enter_context(self.nc.sbuf_tensor("arena", [128, words], F32))
        self.arena_words = words
        self.arena_off = 0
        self.arena_n = 0

    def alloc(self, name, shape, dtype):
        per = 1
        for x in shape[1:]:
            per *= x
        words = per if dtype == F32 else (per + 1) // 2
        words = (words + 7) // 8 * 8
        assert self.arena_off + words <= self.arena_words, (name, self.arena_off, words)
        v = self.arena[0:shape[0], self.arena_off:self.arena_off + words]
        self.arena_off += words
        if dtype != F32:
            v = v.bitcast(dtype)
        v = v[:, 0:per]
        if len(shape) == 3:
            v = v.rearrange("p (a b) -> p a b", b=shape[2])
        self.arena_n += 1
        return Tile(v, f"{name}_{self.arena_n}")

    def ring(self, name, shape, dtype, n):
        return Ring([self.alloc(f"{name}{i}", shape, dtype) for i in range(n)])

    def barrier(self):
        deps = set(self.pending_dma)
        for e in self.ENGS:
            if self.last_real[e] is not None:
                deps.add(self.last_real[e])
        for e in self.ENGS:
            self.fence(e, deps)
        self.pending_dma = []

    def arena_reset(self):
        self.barrier()
        self.arena_off = 0

    def new_sem(self, name):
        self.nsem += 1
        return self.stack.enter_context(self.nc.semaphore(name))

    def sb(self, name, shape, dtype):
        return Tile(self.stack.enter_context(self.nc.sbuf_tensor("sb_" + name, list(shape), dtype)), name)

    def ps(self, name, shape, dtype=F32):
        return Tile(self.stack.enter_context(self.nc.psum_tensor("ps_" + name, list(shape), dtype)), name)

    def op(self, eng, fn, reads=(), writes=(), dma=None):
        o = Op(eng, fn, dma)
        o.idx = self.nops
        self.nops += 1
        reads = [x.b if isinstance(x, Tile) else x for x in reads]
        writes = [x.b if isinstance(x, Tile) else x for x in writes]
        for r in reads:
            if r.w is not None:
                o.deps.add(r.w)
        for w in writes:
            if w.w is not None:
                o.deps.add(w.w)
            for rr in w.r:
                o.deps.add(rr)
        for r in reads:
            r.r.append(o)
        for w in writes:
            w.w = o
            w.r = []
        if eng == "pe" and dma is None:
            o.deps = {d for d in o.deps if not (d.eng == "pe" and d.dma is None)}
        o.deps.discard(o)
        self.ops[eng].append(o)
        if dma is not None:
            self.pending_dma.append(o)
        else:
            self.last_real[eng] = o
        return o

    def fence(self, eng, ops):
        o = Op(eng, None, None)
        o.idx = self.nops
        self.nops += 1
        o.deps = set(ops)
        self.ops[eng].append(o)
        return o

    def mm(self, out, lhsT, rhs, start, stop, r, w):
        return self.op("pe", lambda e: e.matmul(out, lhsT=lhsT, rhs=rhs, start=start, stop=stop), r, w)

    def tr(self, out, in_, ident, r, w):
        return self.op("pe", lambda e: e.transpose(out, in_, ident), r, w)

    def act(self, out, in_, func, r, w, scale=1.0, bias=0.0, accum=None):
        if accum is None:
            return self.op("act", lambda e: e.activation(out=out, in_=in_, func=func, scale=scale, bias=bias), r, w)
        return self.op("act", lambda e: e.activation(out=out, in_=in_, func=func, scale=scale, bias=bias,
                                                     accum_out=accum), r, w)

    def tt(self, eng, out, in0, in1, op, r, w):
        return self.op(eng, lambda e: e.tensor_tensor(out=out, in0=in0, in1=in1, op=op), r, w)

    def ts(self, eng, out, in0, s1, s2, op0, op1, r, w):
        return self.op(eng, lambda e: e.tensor_scalar(out=out, in0=in0, scalar1=s1, scalar2=s2, op0=op0, op1=op1), r, w)

    def stt(self, out, in0, scalar, in1, op0, op1, r, w):
        return self.op("dve", lambda e: e.scalar_tensor_tensor(out=out, in0=in0, scalar=scalar, in1=in1,
                                                               op0=op0, op1=op1), r, w)

    def cp(self, eng, out, in_, r, w):
        if eng == "act":
            return self.op("act", lambda e: e.copy(out=out, in_=in_), r, w)
        return self.op(eng, lambda e: e.tensor_copy(out=out, in_=in_), r, w)

    def memset(self, eng, ap, val, w):
        return self.op(eng, lambda e: e.memset(ap, val), (), w)

    def dma(self, eng, out, in_, r, w, key):
        return self.op(eng, lambda e: e.dma_start(out=out, in_=in_), r, w, dma=key)

    def finalize(self):
        nc = self.nc
        for e in self.ENGS:
            for o in self.ops[e]:
                for d in o.deps:
                    d.signal = True
        ectr = {e: _SemCtr(self, f"s_{e}", 30000) for e in self.ENGS}
        kctr = {}
        for e in self.ENGS:
            for o in self.ops[e]:
                if o.dma is not None:
                    if o.dma not in kctr:
                        kctr[o.dma] = _SemCtr(self, f"d_{o.dma}", 30000)
                    o.sem, o.value = kctr[o.dma].bump(16)
                elif o.signal and o.fn is not None:
                    o.sem, o.value = ectr[e].bump(1)

        def run(eng_name, eng):
            known = {}
            for o in self.ops[eng_name]:
                need = {}
                for d in o.deps:
                    k = id(d.sem)
                    if k not in need or need[k][1] < d.value:
                        need[k] = (d.sem, d.value)
                for k, (sem, v) in need.items():
                    if known.get(k, 0) < v:
                        eng.wait_ge(sem, v)
                        known[k] = v
                if o.fn is None:
                    continue
                ins = o.fn(eng)
                if o.dma is not None:
                    ins.then_inc(o.sem, 16)
                elif o.signal:
                    ins.then_inc(o.sem, 1)

        with nc.Block() as block:
            @block.sync
            def _(sync):
                run("sp", sync)

            @block.tensor
            def _(tensor):
                run("pe", tensor)

            @block.scalar
            def _(scalar):
                run("act", scalar)

            @block.vector
            def _(vector):
                run("dve", vector)

            @block.gpsimd
            def _(gpsimd):
                run("pool", gpsimd)


class Ring:
    def __init__(self, tiles):
        self.tiles = tiles
        self.i = 0

    def next(self):
        t = self.tiles[self.i % len(self.tiles)]
        self.i += 1
        return t


ARENA_WORDS = 25216


class Builder:
    def __init__(self, nc, st, layers=(0, 1)):
        self.nc = nc
        self.P = P = Prog(nc, st)
        self.layers = layers
        self.out_ops = []
        dt = nc.dram_tensor

        def ein(name, shape, dtype=F32):
            return dt(name, list(shape), dtype, kind="ExternalInput").ap()

        self.d = d = {}
        d["cT"] = ein("cT", [128, KC, 2])
        d["wada"] = ein("wada", [2, 48, 128, KC * 128])
        d["badac"] = ein("badac", [2, 128, 48])
        d["ngc"] = ein("ngc", [2, 128, KC])
        d["cst"] = ein("cst", [128, 384])
        d["cstb"] = ein("cstb", [128, 1280])
        if 0 in layers:
            d["x"] = ein("x", [SEQ, D])
            d["ctx"] = ein("ctx", [NCTX, D])
            d["w0a"] = ein("w0a", [8, 5, 128, KC, 128])
            d["w0b"] = ein("w0b", [4, 6, 128, KC, 128])
            d["w0z"] = ein("w0z", [2, 128, KC, 16])
            d["wa2"] = ein("wa2", [16, 2, 512])
            d["ba2c"] = ein("ba2c", [128, 8])
            d["lbl"] = ein("lbl", [128, 3, 8])
            d["g0c"] = ein("g0c", [128, 3])
            d["wo0"] = ein("wo0", [16, 128, KC, 128])
        if 1 in layers:
            d["w1"] = ein("w1", [8, 8, 128, KC, 128])
            d["qkg"] = ein("qkg", [128, 2])
            d["lqb"] = ein("lqb", [128, 4, 128])
            d["cogb"] = ein("cogb", [128, 256])
            d["wo1"] = ein("wo1", [16, 128, KC, 128])
            d["rope"] = ein("rope", [2, 128, SEQ])
        if layers == (0, 1):
            d["x1"] = dt("x1", [SEQ, D], F32, kind="Internal").ap()
            d["ctx1"] = dt("ctx1", [NCTX, D], F32, kind="Internal").ap()
            d["out"] = dt("out", [SEQ, D], F32, kind="ExternalOutput").ap()
        elif layers == (0,):
            d["x1"] = dt("x1", [SEQ, D], F32, kind="ExternalOutput").ap()
            d["ctx1"] = dt("ctx1", [NCTX, D], F32, kind="ExternalOutput").ap()
        else:
            d["x1"] = ein("x1", [SEQ, D])
            d["ctx1"] = ein("ctx1", [NCTX, D])
            d["out"] = dt("out", [SEQ, D], F32, kind="ExternalOutput").ap()
        d["og"] = dt("og", [KC, 128, L], BF16, kind="Internal").ap()
        self.db = {k: Buf("dram_" + k) for k in ("x1", "ctx1", "og", "out")}

        self.hT = P.sb("hT", [128, KC, L], BF16)
        self.hTb = [Buf(f"hT{t}") for t in range(NT)]
        self.hTc = [Buf(f"hTc{t}") for t in range(NT)]
        self.cst = P.sb("cst", [128, 384], F32)
        self.cstb = P.sb("cstb", [128, 1280], BF16)
        self.small = P.sb("small", [128, 8], F32)
        self.modc = P.sb("modc", [128, 48, 2], F32)
        self.Acol = P.sb("Acol", [128, KC, 2], F32)
        self.ngc = P.sb("ngc", [128, KC], F32)
        self.badac = P.sb("badac", [128, 48], F32)
        self.scT = P.sb("scT", [128, KC, 2], F32)
        self.bank = [P.ps(f"bank{i}", [128, 512]) for i in range(8)]
        self.wfree_list = [P.sb(f"wr{i}", [128, KC, 128], BF16) for i in range(8)]
        P.init_arena(ARENA_WORDS)

        P.dma("sp", self.cst[:], d["cst"], [], [self.cst], "cst")
        P.dma("pool", self.cstb[:], d["cstb"], [], [self.cstb], "cstb")
        P.memset("dve", self.small[:, 0:1], EPS, [self.small])
        P.memset("dve", self.small[:, 1:2], 1.0, [self.small])
        P.memset("dve", self.small[:, 2:3], EPS / (1.0 - LAM_INIT) ** 2, [self.small])
        self.ident = self.cst[:, 0:128]
        self.mask = {"f": self.cst[:, 128:256], "b": self.cst[:, 256:384]}
        self.cm4b = self.cstb[:, 0:512]
        self.rotTb = self.cstb[:, 512:640]
        self.onesb = self.cstb[:, 640:768]
        self.mask64 = {"f": self.cstb[:, 768:896], "b": self.cstb[:, 896:1024]}
        self.cm2b = self.cstb[:, 1024:1280]
        self.eps = self.small[:, 0:1]
        self.one = self.small[:, 1:2]

    def wload(self, src):
        t = self.wfree_list.pop(0)
        ncols = src.shape[-1]
        self.P.dma("pool", t[:, :, 0:ncols], src, [], [t], "w_" + t.b.name)
        return t

    def wfree(self, *ts):
        for t in ts:
            if isinstance(t, (list, tuple)):
                self.wfree(*t)
            elif t is not None:
                assert t not in self.wfree_list
                self.wfree_list.append(t)

    def adaln_gen(self, l):
        P, d = self.P, self.d
        P.dma("sp", self.ngc[:], d["ngc"][l], [], [self.ngc], "ngc")
        P.dma("sp", self.badac[:], d["badac"][l], [], [self.badac], "badac")
        if l == self.layers[0]:
            P.dma("sp", self.scT[:], d["cT"], [], [self.scT], "scT")
            P.act(self.scT[:], self.scT[:], AF.Silu, [self.scT], [self.scT])
        row = P.alloc("adrow", [2, 128], F32)
        big = P.ring("wadab", [128, KC * 128], F32, 3)
        pb = Ring([self.bank[4], self.bank[5]])
        wts = {}

        def issue(n):
            wt = big.next()
            P.dma("act", wt[:], d["wada"][l, n], [], [wt], "big_" + wt.b.name)
            wts[n] = wt

        issue(0)
        issue(1)
        for n in range(48):
            if n + 2 < 48:
                issue(n + 2)
            wt = wts.pop(n)
            wv = wt[:].rearrange("p (k j) -> p k j", j=128)
            ps = pb.next()
            for k in range(KC):
                P.mm(ps[0:2, 0:128], self.scT[:, k, :], wv[:, k, :], k == 0, k == KC - 1, [self.scT, wt], [ps])
            P.cp("act", row[:], ps[0:2, 0:128], [ps], [row])
            ps2 = pb.next()
            P.tr(ps2[:, 0:2], row[:], self.ident[0:2, 0:2], [row, self.cst], [ps2])
            P.ts("dve", self.modc[:, n, :], ps2[:, 0:2], self.badac[:, n:n + 1], None, ALU.add, ALU.bypass,
                 [ps2, self.badac], [self.modc])
            yield
        for r in range(2):
            P.stt(self.Acol[:, :, r], self.modc[:, 16:32, r], 1.0, self.ngc[:], ALU.add, ALU.mult,
                  [self.modc, self.ngc], [self.Acol])

    def adaln(self, l):
        for _ in self.adaln_gen(l):
            pass

    def build_hT(self, xsrc, csrc, src_bufs):
        P = self.P
        xn = P.ring("xn", [128, D], F32, 2)
        junk = P.alloc("junk", [128, D], BF16)
        str_ = P.ring("nstat", [128, 4], F32, 2)
        big = P.ring("xin", [128, D], F32, 2)
        pb = Ring([self.bank[0], self.bank[1], self.bank[2], self.bank[3]])
        for t in range(NT):
            r = 1 if t < 2 else 0
            src = csrc[t * 128:(t + 1) * 128, :] if t < 2 else xsrc[(t - 2) * 128:(t - 1) * 128, :]
            xt = big.next()
            st = str_.next()
            xnt = xn.next()
            P.dma("sp", xt[:], src, src_bufs, [xt], "big_" + xt.b.name)
            P.act(junk[:], xt[:], AF.Square, [xt], [junk, st], accum=st[:, 0:1])
            P.act(st[:, 1:2], st[:, 0:1], AF.Sqrt, [st, self.small], [st], scale=1.0 / D, bias=self.eps)
            P.op("dve", lambda e, st=st: e.reciprocal(st[:, 2:3], st[:, 1:2]), [st], [st])
            P.ts("dve", xnt[:], xt[:], st[:, 2:3], None, ALU.mult, ALU.bypass, [xt, st], [xnt])
            for g in range(4):
                ps = pb.next()
                for j in range(4):
                    k = 4 * g + j
                    P.tr(ps[:, 128 * j:128 * j + 128], xnt[:, 128 * k:128 * k + 128], self.ident, [xnt, self.cst], [ps])
                for j in range(4):
                    k = 4 * g + j
                    if g % 2 == 0:
                        P.act(self.hT[:, k, t * 128:(t + 1) * 128], ps[:, 128 * j:128 * j + 128], AF.Identity,
                              [ps, self.Acol, self.modc], [self.hTb[t]], scale=self.Acol[:, k, r:r + 1],
                              bias=self.modc[:, k, r:r + 1])
                    else:
                        P.ts("dve", self.hT[:, k, t * 128:(t + 1) * 128], ps[:, 128 * j:128 * j + 128],
                             self.Acol[:, k, r:r + 1], self.modc[:, k, r:r + 1], ALU.mult, ALU.add,
                             [ps, self.Acol, self.modc], [self.hTc[t]])

    def hr(self, s0, bw):
        return self.hTb[s0 // 128:(s0 + bw) // 128] + self.hTc[s0 // 128:(s0 + bw) // 128]

    def out_proj(self, wsrc, xsrc, csrc, src_bufs, xdst, cdst, dst_bufs, tok0, is_out, side=None):
        P, d = self.P, self.d
        G = self.hT
        Gb = [Buf(f"Gt{t}") for t in range(NT)]
        for bi, (s0, bw) in enumerate(BLOCKS):
            if s0 < tok0:
                continue
            P.dma("sp" if bi % 2 == 0 else "act", G[:, :, s0:s0 + bw], d["og"][:, :, s0:s0 + bw].rearrange("k p t -> p k t"),
                  [self.db["og"]], Gb[s0 // 128:(s0 + bw) // 128], f"G{bi}")
        pb = Ring([self.bank[0], self.bank[1], self.bank[2], self.bank[3]])
        nr = 2 if tok0 == 0 else 1
        self.gbc = [P.alloc(f"gbc{r}", [128, D], F32) for r in range(nr)]
        gtmp = P.ring("gtmp", [128, 128], F32, 2)
        for r in range(nr):
            for k in range(KC):
                tmp = gtmp.next()
                P.cp("dve", tmp[:], self.modc[:, 32 + k, r:r + 1].to_broadcast([128, 128]), [self.modc], [tmp])
                ps = pb.next()
                P.tr(ps[:, 0:128], tmp[:], self.ident, [tmp, self.cst], [ps])
                P.cp("act", self.gbc[r][:, 128 * k:128 * k + 128], ps[:, 0:128], [ps], [self.gbc[r]])
        xr = P.ring("xo", [128, 512], F32, 3)
        tr_ = P.ring("xt", [128, 512], F32, 3)
        wts_next = [self.wload(wsrc[j]) for j in range(4)]
        for n in range(4):
            wts = wts_next
            if n + 1 < 4:
                wts_next = [self.wload(wsrc[4 * (n + 1) + j]) for j in range(4)]
            for t in range(tok0 // 128, NT):
                r = 1 if t < 2 else 0
                if t < 2:
                    src = csrc[t * 128:(t + 1) * 128, n * 512:(n + 1) * 512]
                    dst = cdst[t * 128:(t + 1) * 128, n * 512:(n + 1) * 512]
                else:
                    src = xsrc[(t - 2) * 128:(t - 1) * 128, n * 512:(n + 1) * 512]
                    dst = xdst[(t - 2) * 128:(t - 1) * 128, n * 512:(n + 1) * 512]
                xt = xr.next()
                P.dma("sp", xt[:], src, src_bufs, [xt], "xo_" + xt.b.name)
                ps = pb.next()
                for j, wt in enumerate(wts):
                    for k in range(KC):
                        P.mm(ps[:, 128 * j:128 * j + 128], G[:, k, t * 128:(t + 1) * 128], wt[:, k, :],
                             k == 0, k == KC - 1, [Gb[t], wt], [ps])
                tmp = tr_.next()
                P.tt("dve", tmp[:], ps[:], self.gbc[r][:, n * 512:(n + 1) * 512], ALU.mult, [ps, self.gbc[r]], [tmp])
                P.tt("pool", tmp[:], tmp[:], xt[:], ALU.add, [tmp, xt], [tmp])
                o = P.dma("pool", dst, tmp[:], [tmp], dst_bufs, "xs_" + tmp.b.name)
                if is_out:
                    self.out_ops.append(o)
                if side is not None:
                    try:
                        next(side)
                    except StopIteration:
                        side = None
            self.wfree(wts)
        if side is not None:
            for _ in side:
                pass

    def layer0(self):
        P, d = self.P, self.d
        self.adaln(0)
        P.arena_reset()
        self.build_hT(d["x"], d["ctx"], [])
        P.arena_reset()
        hT = self.hT
        lbl = P.alloc("lbl", [128, 3, 8], F32)
        lbt = P.alloc("lbt", [128, 4, 8], F32)
        P.dma("sp", lbl[:], d["lbl"], [], [lbl], "lbl")
        P.act(lbl[:], lbl[:], AF.Exp, [lbl], [lbl])
        P.tt("dve", lbt[:, 3, :], lbl[:, 0, :], lbl[:, 1, :], ALU.add, [lbl], [lbt])
        P.tt("dve", lbt[:, 3, :], lbt[:, 3, :], lbl[:, 2, :], ALU.add, [lbl, lbt], [lbt])
        P.op("dve", lambda e: e.reciprocal(lbt[:, 3, :], lbt[:, 3, :]), [lbt], [lbt])
        P.tt("dve", lbt[:, 0, :], lbl[:, 0, :], lbt[:, 3, :], ALU.mult, [lbl, lbt], [lbt])
        P.ts("dve", lbt[:, 1, :], lbt[:, 0, :], -1.0, 1.0, ALU.mult, ALU.add, [lbt], [lbt])
        P.ts("dve", lbt[:, 2, :], lbt[:, 1, :], -1.0, None, ALU.mult, ALU.bypass, [lbt], [lbt])
        g0c = P.alloc("g0c", [128, 3], F32)
        P.dma("sp", g0c[:], d["g0c"], [], [g0c], "g0c")
        ba2 = P.alloc("ba2", [128, 8], F32)
        P.dma("sp", ba2[:], d["ba2c"], [], [ba2], "ba2")
        P.ts("dve", ba2[:], ba2[:], -1.0, None, ALU.mult, ALU.bypass, [ba2], [ba2])
        wa2 = P.alloc("wa2", [16, 2, 128], F32)
        wz = [P.alloc(f"wz{i}", [128, KC, 16], BF16) for i in range(2)]
        for i in range(2):
            P.dma("pool", wz[i][:], d["w0z"][i], [], [wz[i]], f"wz{i}")

        qs = P.alloc("qs", [128, L], BF16)
        kk = P.alloc("kk", [128, L], BF16)
        sg = [P.alloc("sg0", [128, L], BF16), P.alloc("sg1", [128, L], BF16)]
        vv = P.alloc("vv", [128, NT, 256], BF16)
        vvb = [Buf(f"vv{t}") for t in range(NT)]
        oacc = [P.alloc("oacc0", [128, L], F32), P.alloc("oacc1", [128, L], F32)]
        oab = [[Buf(f"oa{vc}_{t}") for t in range(NT)] for vc in range(2)]
        pb = Ring([self.bank[0], self.bank[1]])
        ftmp = P.ring("ftmp", [128, 520], F32, 5)
        btmp = P.ring("btmp", [128, 512], BF16, 3)
        zs = P.ring("zs", [16, 512], F32, 1)

        def proj_fm(wap, ncol, blk, rd):
            s0, bw = BLOCKS[blk]
            ps = pb.next()
            for k in range(KC):
                P.mm(ps[0:ncol, 0:bw], wap[:, k, 0:ncol], hT[:, k, s0:s0 + bw], k == 0, k == KC - 1,
                     rd + self.hr(s0, bw), [ps])
            return ps

        W = {}
        for dr in "fb":
            W[dr] = dict(
                qd=P.ring(f"qd{dr}", [128, 512], BF16, 2), ki=P.ring(f"ki{dr}", [128, 512], BF16, 2),
                ktT=P.ring(f"ktT{dr}", [128, 512], F32, 2), gt=P.ring(f"gt{dr}", [128, 16], F32, 2),
                kt=P.ring(f"kt{dr}", [128, 128], BF16, 2), ktm=P.ring(f"ktm{dr}", [128, 512], BF16, 3),
                scm=P.ring(f"scm{dr}", [128, 128], BF16, 2),
                S32=P.ring(f"S32{dr}", [128, 256], F32, 3), Sbf=P.ring(f"Sbf{dr}", [128, 256], BF16, 5),
            )
        PB = {"f": (self.bank[2], self.bank[3], self.bank[4]), "b": (self.bank[5], self.bank[6], self.bank[7])}

        def elementwise(dr, blk, is_a, wdir, hd):
            s0, bw = BLOCKS[blk]
            CS = 32 if is_a else 64
            ncb = bw // CS
            w = W[dr]
            f1, f2, Fp, f3 = ftmp.next(), ftmp.next(), ftmp.next(), ftmp.next()
            qd, ki, ktT, gt = w["qd"].next(), w["ki"].next(), w["ktT"].next(), w["gt"].next()
            if is_a:
                ls = 1.0
                ps = proj_fm(wdir, 128, blk, [wdir])
                P.act(f1[:, 0:bw], ps[:, 0:bw], AF.Sigmoid, [ps], [f1])
                kb = btmp.next()
                P.ts("pool", kb[:, 0:bw], f1[:, 0:bw], lbt[:, 2, hd:hd + 1], lbt[:, 1, hd:hd + 1], ALU.mult, ALU.add,
                     [f1, lbt], [kb])
                ksrc, kbuf = kb[:, 0:bw], kb
                P.act(f2[:, 0:bw], f1[:, 0:bw], AF.Ln, [f1, lbt], [f2], scale=lbt[:, 1, hd:hd + 1],
                      bias=lbt[:, 0, hd:hd + 1])
            else:
                ls = -1.0 / 16.0
                di = 0 if dr == "f" else 1
                ps = proj_fm(wz[di], 16, blk, [wz[di]])
                z = zs.next()
                P.cp("act", z[:, 0:bw], ps[0:16, 0:bw], [ps], [z])
                ps = pb.next()
                P.mm(ps[:, 0:bw], wa2[:, di, :], z[:, 0:bw], True, True, [wa2, z], [ps])
                P.act(f1[:, 0:bw], ps[:, 0:bw], AF.Exp, [ps, ba2], [f1], scale=-1.0,
                      bias=ba2[:, di * 4 + hd:di * 4 + hd + 1])
                P.act(f2[:, 0:bw], f1[:, 0:bw], AF.Ln, [f1, self.small], [f2], scale=1.0, bias=self.one)
                ksrc, kbuf = kk[:, s0:s0 + bw], kk
            P.memset("dve", Fp[:, 0:1], 0.0, [Fp])
            P.op("dve", lambda e: e.tensor_tensor_scan(out=Fp[:, 1:1 + bw], data0=f2[:, 0:bw], data1=f2[:, 0:bw],
                                                       initial=0.0, op0=ALU.add, op1=ALU.bypass), [f2], [Fp])
            c3 = lambda ap: ap.rearrange("p (c j) -> p c j", j=CS)
            Fin, Fex = c3(Fp[:, 1:1 + bw]), c3(Fp[:, 0:bw])
            Rb = Fex[:, :, 0:1].to_broadcast([128, ncb, CS])
            Eb = Fin[:, :, CS - 1:CS].to_broadcast([128, ncb, CS])
            if dr == "f":
                P.tt("dve", c3(f1[:, 0:bw]), Fin, Rb, ALU.subtract, [Fp], [f1])
                P.tt("pool", c3(f3[:, 0:bw]), Eb, Fin, ALU.subtract, [Fp], [f3])
            else:
                P.tt("dve", c3(f1[:, 0:bw]), Eb, Fex, ALU.subtract, [Fp], [f1])
                P.tt("pool", c3(f3[:, 0:bw]), Fex, Rb, ALU.subtract, [Fp], [f3])
            P.tt("dve", gt[:, 0:ncb], Fin[:, :, CS - 1], Fex[:, :, 0], ALU.subtract, [Fp], [gt])
            P.act(gt[:, 0:ncb], gt[:, 0:ncb], AF.Exp, [gt], [gt], scale=ls)
            P.act(f2[:, 0:bw], f1[:, 0:bw], AF.Exp, [f1], [f2], scale=-ls)
            P.act(f1[:, 0:bw], f1[:, 0:bw], AF.Exp, [f1], [f1], scale=ls)
            P.act(f3[:, 0:bw], f3[:, 0:bw], AF.Exp, [f3], [f3], scale=ls)
            P.tt("dve", qd[:, 0:bw], qs[:, s0:s0 + bw], f1[:, 0:bw], ALU.mult, [qs, f1], [qd])
            P.tt("pool", ki[:, 0:bw], ksrc, f2[:, 0:bw], ALU.mult, [kbuf, f2], [ki])
            P.tt("pool", ktT[:, 0:bw], ksrc, f3[:, 0:bw], ALU.mult, [kbuf, f3], [ktT])
            return qd, ki, ktT, gt

        def chain(dr, is_a, wdir, hd, written):
            w = W[dr]
            nvc = 1 if is_a else 2
            dv = 128 * nvc
            po_b, U_b, sc_b = PB[dr]
            S32 = w["S32"].next()
            Sb = w["Sbf"].next()
            P.memset("pool", S32[:, 0:dv], 0.0, [S32])
            P.memset("pool", Sb[:, 0:dv], 0.0, [Sb])
            blocks = [0, 1, 2, 3, 4] if dr == "f" else [0, 4, 3, 2, 1]
            CS = 32 if is_a else 64
            NS = 128 // CS
            corder = list(range(NS)) if dr == "f" else list(range(NS))[::-1]
            cmb = self.cm4b if is_a else self.cm2b
            msk = self.mask[dr] if is_a else self.mask64[dr]
            ew_next = elementwise(dr, blocks[0], is_a, wdir, hd)
            for bi, blk in enumerate(blocks):
                s0, bw = BLOCKS[blk]
                qd, ki, ktT, gt = ew_next
                yield
                tl = list(range(bw // 128))
                if dr == "b":
                    tl = tl[::-1]

                def prep(ti):
                    P.tr(sc_b[:, 128:256], ktT[:, ti * 128:ti * 128 + 128], self.ident, [ktT, self.cst], [sc_b])
                    kt = w["kt"].next()
                    P.cp("act", kt[:], sc_b[:, 128:256], [sc_b], [kt])
                    ktm_ = w["ktm"].next()
                    P.tt("pool", ktm_[:, 0:NS * 128].rearrange("p (c d) -> p c d", d=128),
                         kt[:].rearrange("p (o d) -> p o d", o=1).to_broadcast([128, NS, 128]),
                         cmb.rearrange("p (c d) -> p c d", d=128), ALU.mult, [kt, self.cstb], [ktm_])
                    return ktm_

                ktm_next = prep(tl[0])
                for idx, ti in enumerate(tl):
                    tg = s0 // 128 + ti
                    tsl = slice(ti * 128, ti * 128 + 128)
                    ktm = ktm_next
                    P.mm(sc_b[:, 0:128], ki[:, tsl], qd[:, tsl], True, True, [ki, qd], [sc_b])
                    scm = w["scm"].next()
                    P.tt("dve", scm[:], sc_b[:, 0:128], msk, ALU.mult, [sc_b, self.cst, self.cstb], [scm])
                    Ss = [Sb]
                    for half in range(1):
                        cs = corder
                        for i, c in enumerate(cs):
                            P.mm(U_b[:, i * dv:(i + 1) * dv], ktm[:, c * 128:(c + 1) * 128], vv[:, tg, 0:dv],
                                 True, True, [ktm, vvb[tg]], [U_b])
                        for i, c in enumerate(cs):
                            gcol = gt[:, ti * NS + c:ti * NS + c + 1]
                            Sn = w["S32"].next()
                            P.stt(Sn[:, 0:dv], S32[:, 0:dv], gcol, U_b[:, i * dv:(i + 1) * dv], ALU.mult, ALU.add,
                                  [S32, gt, U_b], [Sn])
                            S32 = Sn
                            Sb = w["Sbf"].next()
                            P.cp("act", Sb[:, 0:dv], S32[:, 0:dv], [S32], [Sb])
                            Ss.append(Sb)
                    if idx + 1 < len(tl):
                        ktm_next = prep(tl[idx + 1])
                    if idx == 0 and bi + 1 < len(blocks):
                        ew_next = elementwise(dr, blocks[bi + 1], is_a, wdir, hd)
                    yield
                    for vc in range(nvc):
                        pos = po_b[:, vc * 128:(vc + 1) * 128]
                        P.mm(pos, vv[:, tg, vc * 128:(vc + 1) * 128], scm[:], True, False, [vvb[tg], scm], [po_b])
                        for i, c in enumerate(corder):
                            P.mm(po_b[:, vc * 128 + CS * c:vc * 128 + CS * c + CS], Ss[i][:, vc * 128:(vc + 1) * 128],
                                 qd[:, ti * 128 + CS * c:ti * 128 + CS * c + CS], False, i == NS - 1, [Ss[i], qd],
                                 [po_b])
                        dst = oacc[vc][:, tg * 128:(tg + 1) * 128]
                        if (vc, tg) not in written:
                            written.add((vc, tg))
                            P.cp("act", dst, pos, [po_b], [oab[vc][tg]])
                        else:
                            P.tt("dve", dst, pos, dst, ALU.add, [po_b, oab[vc][tg]], [oab[vc][tg]])
                    yield

        def finish_head(nvc, gcol0, mix0):
            dv = 128 * nvc
            for blk in range(5):
                s0, bw = BLOCKS[blk]
                ob = lambda vc: oab[vc][s0 // 128:(s0 + bw) // 128]
                ps = pb.next()
                for vc in range(nvc):
                    sq = btmp.next()
                    P.act(sq[:, 0:bw], oacc[vc][:, s0:s0 + bw], AF.Square, ob(vc), [sq])
                    P.mm(ps[:, 0:bw], self.onesb, sq[:, 0:bw], vc == 0, vc == nvc - 1, [self.cstb, sq], [ps])
                rs = ftmp.next()
                P.act(rs[:, 0:bw], ps[:, 0:bw], AF.Ln, [ps, self.small], [rs], scale=1.0 / dv, bias=self.eps)
                P.act(rs[:, 0:bw], rs[:, 0:bw], AF.Exp, [rs], [rs], scale=-0.5)
                for vc in range(nvc):
                    t1 = ftmp.next()
                    P.stt(t1[:, 0:bw], oacc[vc][:, s0:s0 + bw], g0c[:, gcol0 + vc:gcol0 + vc + 1], rs[:, 0:bw],
                          ALU.mult, ALU.mult, ob(vc) + [g0c, rs], [t1])
                    og = btmp.next()
                    P.tt("pool", og[:, 0:bw], t1[:, 0:bw], sg[vc][:, s0:s0 + bw], ALU.mult, [t1, sg[vc]], [og])
                    P.dma("sp", d["og"][mix0 + vc, :, s0:s0 + bw], og[:, 0:bw], [og], [self.db["og"]],
                          "ogs_" + og.b.name)

        def load_head(is_a, hd):
            if is_a:
                wq, wf, wb = (self.wload(d["w0a"][hd, i]) for i in range(3))
                wv = [self.wload(d["w0a"][hd, 3])]
                wg = [self.wload(d["w0a"][hd, 4])]
                return wq, None, wf, wb, wv, wg
            wq = self.wload(d["w0b"][hd, 0])
            wk = self.wload(d["w0b"][hd, 1])
            wv = [self.wload(d["w0b"][hd, 2]), self.wload(d["w0b"][hd, 3])]
            wg = [self.wload(d["w0b"][hd, 4]), self.wload(d["w0b"][hd, 5])]
            return wq, wk, None, None, wv, wg

        def run_head(is_a, hd, wts, nxt):
            nvc = 1 if is_a else 2
            dv = 128 * nvc
            wq, wk, wf, wb, wv, wg = wts
            if not is_a:
                P.dma("sp", wa2[:], d["wa2"][:, :, hd * 128:(hd + 1) * 128], [], [wa2], "wa2")
            for blk in range(5):
                s0, bw = BLOCKS[blk]
                ps = proj_fm(wq, 128, blk, [wq])
                if is_a:
                    P.act(qs[:, s0:s0 + bw], ps[:, 0:bw], AF.Silu, [ps], [qs])
                else:
                    P.act(qs[:, s0:s0 + bw], ps[:, 0:bw], AF.Identity, [ps], [qs], scale=128.0 ** -0.5)
                    ps = proj_fm(wk, 128, blk, [wk])
                    P.cp("act", kk[:, s0:s0 + bw], ps[:, 0:bw], [ps], [kk])
                for vc in range(nvc):
                    ps = proj_fm(wg[vc], 128, blk, [wg[vc]])
                    P.act(sg[vc][:, s0:s0 + bw], ps[:, 0:bw], AF.Silu, [ps], [sg[vc]])
            written = set()
            gens = [chain("f", is_a, wf, hd, written), chain("b", is_a, wb, hd, written)]
            for g in gens:
                next(g)
            for tg in range(NT):
                ps = pb.next()
                for vc in range(nvc):
                    for k in range(KC):
                        P.mm(ps[:, vc * 128:(vc + 1) * 128], hT[:, k, tg * 128:(tg + 1) * 128], wv[vc][:, k, :],
                             k == 0, k == KC - 1, [self.hTb[tg], self.hTc[tg], wv[vc]], [ps])
                P.cp("act", vv[:, tg, 0:dv], ps[:, 0:dv], [ps], [vvb[tg]])
            self.wfree(wq, wk, wv, wg)
            nwts = load_head(*nxt) if nxt is not None else None
            while gens:
                for g in list(gens):
                    try:
                        next(g)
                    except StopIteration:
                        gens.remove(g)
            self.wfree(wf, wb)
            if is_a:
                finish_head(1, 0, hd)
            else:
                finish_head(2, 1, 8 + 2 * hd)
            return nwts

        heads = [(True, h) for h in self.heads_a] + [(False, h) for h in self.heads_b]
        wts = load_head(*heads[0])
        for i, (is_a, hd) in enumerate(heads):
            wts = run_head(is_a, hd, wts, heads[i + 1] if i + 1 < len(heads) else None)
        P.arena_reset()
        side = self.adaln_gen(1) if 1 in self.layers else None
        self.out_proj(d["wo0"], d["x"], d["ctx"], [], d["x1"], d["ctx1"], [self.db["x1"], self.db["ctx1"]], 0,
                      self.layers == (0,), side=side)
        P.arena_reset()

    heads_a = range(8)
    heads_b = range(4)

    def layer1(self):
        P, d = self.P, self.d
        srcb = [self.db["x1"], self.db["ctx1"]]
        if 0 not in self.layers:
            self.adaln(1)
            P.arena_reset()
        self.build_hT(d["x1"], d["ctx1"], srcb)
        P.arena_reset()
        hT = self.hT
        qkg = P.alloc("qkg", [128, 2], F32)
        P.dma("sp", qkg[:], d["qkg"], [], [qkg], "qkg")
        cogb = P.alloc("cogb", [128, 256], F32)
        P.dma("sp", cogb[:], d["cogb"], [], [cogb], "cogb")
        lq = P.alloc("lq", [128, 4, 128], F32)
        lam = P.alloc("lam", [128, 4], F32)
        P.dma("sp", lq[:], d["lqb"], [], [lq], "lq")
        P.tt("dve", lq[:, 0, :], lq[:, 0, :], lq[:, 1, :], ALU.mult, [lq], [lq])
        P.tt("dve", lq[:, 2, :], lq[:, 2, :], lq[:, 3, :], ALU.mult, [lq], [lq])
        P.op("dve", lambda e: e.reduce_sum(lam[:, 0:1], lq[:, 0, :], axis=AX.X), [lq], [lam])
        P.op("dve", lambda e: e.reduce_sum(lam[:, 1:2], lq[:, 2, :], axis=AX.X), [lq], [lam])
        P.act(lam[:, 0:2], lam[:, 0:2], AF.Exp, [lam], [lam])
        P.tt("dve", lam[:, 2:3], lam[:, 0:1], lam[:, 1:2], ALU.subtract, [lam], [lam])
        P.ts("dve", lam[:, 2:3], lam[:, 2:3], LAM_INIT, None, ALU.add, ALU.bypass, [lam], [lam])
        rope = P.alloc("rope", [128, 2, SEQ], F32)
        P.dma("sp", rope[:, 0, :], d["rope"][0], [], [rope], "rope")
        P.dma("sp", rope[:, 1, :], d["rope"][1], [], [rope], "rope")

        qT = [P.alloc(f"qT{i}", [128, SEQ], BF16) for i in range(2)]
        kT = [P.alloc(f"kT{i}", [128, L], BF16) for i in range(2)]
        sgT = [P.alloc(f"sgT{i}", [128, SEQ], BF16) for i in range(2)]
        va = P.alloc("va", [128, NT, 258], BF16)
        vab = [Buf(f"va{t}") for t in range(NT)]
        P.memset("pool", va[:, :, 256:257], 1.0, vab)
        ogh = P.alloc("ogh", [128, 2, SEQ], BF16)
        pb = Ring([self.bank[0], self.bank[1]])
        sq_r = P.ring("sq", [128, 512], BF16, 2)
        rs_r = P.ring("rs", [128, 512], F32, 2)
        qn_r = P.ring("qn", [128, 512], BF16, 2)
        t1_r = P.ring("t1", [128, 512], F32, 2)
        t2_r = P.ring("t2", [128, 512], F32, 2)
        po = [[self.bank[4], self.bank[5]], [self.bank[6], self.bank[7]]]
        on_r = P.ring("on", [128, 256], F32, 2)
        ot_r = P.ring("ot", [128, 256], F32, 2)
        st_r = P.ring("ast", [128, 8], F32, 2)
        junk = P.alloc("junk1", [128, 256], BF16)
        mhalf = P.alloc("mhalf", [128, 2], F32)
        P.memset("pool", mhalf[:], -0.5, [mhalf])

        slots = [(self.bank[0], self.bank[1]), (self.bank[2], self.bank[3]), (self.bank[4], self.bank[5])]
        vbank = Ring([self.bank[6], self.bank[7]])
        sT_t = [self.bank[1], self.bank[2], self.bank[3]]
        pbe = Ring([self.bank[0]])
        eT_r = P.ring("eT4", [128, 512], BF16, 4)
        pc_r = P.ring("pcopy", [128, 4, 257], F32, 2)

        def proj16(ps, wt, s0, bw):
            for k in range(KC):
                P.mm(ps[:, 0:bw], wt[:, k, :], hT[:, k, s0:s0 + bw], k == 0, k == KC - 1, [wt] + self.hr(s0, bw), [ps])

        def task_qk(slot, wt, s0, bw, gcol, dst, dcol, rope_t0):
            psA, psB = slots[slot]
            proj16(psA, wt, s0, bw)
            sq = sq_r.next()
            P.act(sq[:, 0:bw], psA[:, 0:bw], AF.Square, [psA], [sq])
            yield
            P.mm(psB[:, 0:bw], self.onesb, sq[:, 0:bw], True, True, [self.cstb, sq], [psB])
            rs = rs_r.next()
            P.act(rs[:, 0:bw], psB[:, 0:bw], AF.Ln, [psB, self.small], [rs], scale=1.0 / 128, bias=self.eps)
            P.act(rs[:, 0:bw], rs[:, 0:bw], AF.Exp, [rs], [rs], scale=-0.5)
            if rope_t0 is None:
                P.stt(dst[:, dcol:dcol + bw], psA[:, 0:bw], gcol, rs[:, 0:bw], ALU.mult, ALU.mult, [psA, qkg, rs], [dst])
                return
            qn = qn_r.next()
            P.stt(qn[:, 0:bw], psA[:, 0:bw], gcol, rs[:, 0:bw], ALU.mult, ALU.mult, [psA, qkg, rs], [qn])
            yield
            P.mm(psB[:, 0:bw], self.rotTb, qn[:, 0:bw], True, True, [self.cstb, qn], [psB])
            t1, t2 = t1_r.next(), t2_r.next()
            P.tt("pool", t1[:, 0:bw], qn[:, 0:bw], rope[:, 0, rope_t0:rope_t0 + bw], ALU.mult, [qn, rope], [t1])
            P.tt("dve", t2[:, 0:bw], psB[:, 0:bw], rope[:, 1, rope_t0:rope_t0 + bw], ALU.mult, [psB, rope], [t2])
            P.tt("pool", dst[:, dcol:dcol + bw], t1[:, 0:bw], t2[:, 0:bw], ALU.add, [t1, t2], [dst])

        def task_g(slot, wt, s0, bw, dst):
            psA, psB = slots[slot]
            proj16(psA, wt, s0, bw)
            P.act(dst[:, s0 - NCTX:s0 - NCTX + bw], psA[:, 0:bw], AF.Silu, [psA], [dst])
            return
            yield

        def task_v(slot, wv, tg):
            ps = vbank.next()
            for vc in range(2):
                for k in range(KC):
                    P.mm(ps[:, vc * 128:(vc + 1) * 128], hT[:, k, tg * 128:(tg + 1) * 128], wv[vc][:, k, :],
                         k == 0, k == KC - 1, [self.hTb[tg], self.hTc[tg], wv[vc]], [ps])
            P.cp("act", va[:, tg, 0:256], ps[:, 0:256], [ps], [vab[tg]])
            return
            yield

        def run_tasks(tasks, nslots):
            pending = list(tasks)
            active = {}
            while pending or active:
                for sl in range(nslots):
                    if sl not in active and pending:
                        active[sl] = pending.pop(0)(sl)
                    if sl in active:
                        try:
                            next(active[sl])
                        except StopIteration:
                            del active[sl]

        hc = list(self.heads_c)
        w1n = [self.wload(d["w1"][hc[0], i]) for i in (2, 0, 6, 4, 5, 3, 1, 7)]
        for hi, hd in enumerate(hc):
            wk0, wq0, wg0, wv0, wv1, wk1, wq1, wg1 = w1n
            wq, wk, wv, wg = [wq0, wq1], [wk0, wk1], [wv0, wv1], [wg0, wg1]
            tasks = []
            vt = list(range(NT))
            for n in range(2):
                for blk in range(5):
                    s0, bw = BLOCKS[blk]
                    tasks.append(lambda sl, n=n, s0=s0, bw=bw, blk=blk: task_qk(
                        sl, wk[n], s0, bw, qkg[:, 1:2], kT[n], s0, None if blk == 0 else s0 - NCTX))
                    if blk > 0:
                        tasks.append(lambda sl, n=n, s0=s0, bw=bw: task_qk(
                            sl, wq[n], s0, bw, qkg[:, 0:1], qT[n], s0 - NCTX, s0 - NCTX))
                        tasks.append(lambda sl, n=n, s0=s0, bw=bw: task_g(sl, wg[n], s0, bw, sgT[n]))
                    for _ in range(2):
                        if vt:
                            tg = vt.pop(0)
                            tasks.append(lambda sl, tg=tg: task_v(sl, wv, tg))
            while vt:
                tg = vt.pop(0)
                tasks.append(lambda sl, tg=tg: task_v(sl, wv, tg))
            run_tasks(tasks, 3)
            self.wfree(w1n)
            if hi + 1 < len(hc):
                w1n = [self.wload(d["w1"][hc[hi + 1], i]) for i in (2, 0, 6, 4, 5, 3, 1, 7)]
            LOOK = 2
            steps = [(qb, kt) for qb in range(SEQ // 256) for kt in range(NT)]
            eTs = {}
            deferred = []
            for i in range(len(steps) + LOOK):
                while deferred and deferred[0][0] <= i:
                    deferred.pop(0)[1]()
                if i < len(steps):
                    qb, kt = steps[i]
                    q0 = qb * 256
                    sT = sT_t[i % 3]
                    for n in range(2):
                        P.mm(sT[:, n * 256:(n + 1) * 256], kT[n][:, kt * 128:(kt + 1) * 128], qT[n][:, q0:q0 + 256],
                             True, True, [kT[n], qT[n]], [sT])
                    eT = eT_r.next()
                    P.act(eT[:], sT[:], AF.Exp, [sT], [eT], scale=128.0 ** -0.5)
                    eTs[i] = eT
                if i - LOOK < 0:
                    continue
                qb, kt = steps[i - LOOK]
                eT = eTs.pop(i - LOOK)
                q0 = qb * 256
                for n in range(2):
                    for j in range(2):
                        P.mm(po[n][j][:, 0:257], eT[:, n * 256 + j * 128:n * 256 + (j + 1) * 128], va[:, kt, 0:257],
                             kt == 0, kt == NT - 1, [eT, vab[kt]], [po[n][j]])
                if kt != NT - 1:
                    continue
                pc = pc_r.next()
                for n2 in range(2):
                    for j in range(2):
                        P.cp("dve", pc[:, 2 * n2 + j, :], po[n2][j][:, 0:257], [po[n2][j]], [pc])
                for j in range(2):
                    st = st_r.next()
                    P.op("dve", lambda e, st=st, pc=pc, j=j: e.reciprocal(st[:, 0:1], pc[:, j, 256:257]), [pc], [st])
                    P.op("dve", lambda e, st=st, pc=pc, j=j: e.reciprocal(st[:, 1:2], pc[:, 2 + j, 256:257]), [pc], [st])
                    P.tt("dve", st[:, 1:2], st[:, 1:2], lam[:, 2:3], ALU.mult, [st, lam], [st])
                    ot = ot_r.next()
                    P.ts("pool", ot[:], pc[:, 2 + j, 0:256], st[:, 1:2], None, ALU.mult, ALU.bypass, [pc, st], [ot])
                    on = on_r.next()
                    P.stt(on[:], pc[:, j, 0:256], st[:, 0:1], ot[:], ALU.mult, ALU.subtract, [pc, st, ot], [on])
                    P.op("dve", lambda e, on=on, st=st, ot=ot: e.scalar_tensor_tensor(
                        out=ot[:], in0=on[:], scalar=1.0, in1=on[:], op0=ALU.mult, op1=ALU.mult,
                        accum_out=st[:, 2:3]), [on], [ot, st])
                    P.ts("dve", st[:, 3:4], st[:, 2:3], 1.0 / (256.0 * (1.0 - LAM_INIT) ** 2),
                         EPS / (1.0 - LAM_INIT) ** 2, ALU.mult, ALU.add, [st], [st])
                    P.tt("pool", st[:, 4:5], st[:, 3:4], mhalf[:, 0:1], ALU.pow, [st, mhalf], [st])
                    P.stt(on[:], on[:], st[:, 4:5], cogb[:], ALU.mult, ALU.mult, [on, st, cogb], [on])

                    def fin(on=on, tq=q0 + j * 128):
                        ps = pbe.next()
                        for vc in range(2):
                            P.tr(ps[:, vc * 128:(vc + 1) * 128], on[:, vc * 128:(vc + 1) * 128], self.ident,
                                 [on, self.cst], [ps])
                        for vc in range(2):
                            P.tt("dve", ogh[:, vc, tq:tq + 128], ps[:, vc * 128:(vc + 1) * 128],
                                 sgT[vc][:, tq:tq + 128], ALU.mult, [ps, sgT[vc]], [ogh])
                    deferred.append((i + 3 + j, fin))
            for _, fin in deferred:
                fin()
            deferred = []
            for vc in range(2):
                P.dma("sp", d["og"][2 * hd + vc, :, NCTX:L], ogh[:, vc, :], [ogh], [self.db["og"]], f"ogh{vc}")
        P.arena_reset()
        self.out_proj(d["wo1"], d["x1"], d["ctx1"], srcb, d["out"], None, [self.db["out"]], NCTX, True)

    heads_c = range(8)

    def build(self):
        if 0 in self.layers:
            self.layer0()
        if 1 in self.layers:
            self.layer1()
        self.P.fence("sp", self.out_ops)
        self.P.finalize()


def _lay(w, col0, ncol):
    return np.ascontiguousarray(w[:, col0:col0 + ncol].reshape(KC, 128, ncol).transpose(1, 0, 2))


def _consts():
    c = np.zeros((128, 384), np.float32)
    cb = np.zeros((128, 1280), np.float32)
    i = np.arange(128)
    c[:, 0:128] = np.eye(128, dtype=np.float32)
    same = (i[:, None] // 32) == (i[None, :] // 32)
    c[:, 128:256] = (same & (i[:, None] <= i[None, :]))
    c[:, 256:384] = (same & (i[:, None] >= i[None, :]))
    for ch in range(4):
        cb[:, ch * 128:(ch + 1) * 128] = ((i // 32) == ch)[:, None]
    rot = np.zeros((128, 128), np.float32)
    for m in range(128):
        if (m % 64) < 32:
            rot[m + 32, m] = -1.0
        else:
            rot[m - 32, m] = 1.0
    cb[:, 512:640] = rot
    cb[:, 640:768] = 1.0
    same64 = (i[:, None] // 64) == (i[None, :] // 64)
    cb[:, 768:896] = (same64 & (i[:, None] <= i[None, :]))
    cb[:, 896:1024] = (same64 & (i[:, None] >= i[None, :]))
    for ch in range(2):
        cb[:, 1024 + ch * 128:1024 + (ch + 1) * 128] = ((i // 64) == ch)[:, None]
    return c, cb


def _rope_tables():
    half = 32
    inv = (10000.0 ** (-np.arange(half, dtype=np.float32) / half)).astype(np.float32)
    t = np.arange(SEQ)
    pos_r = (t // 64).astype(np.float32)
    pos_c = (t % 64).astype(np.float32)
    dd = np.arange(128)
    pos = np.where((dd // 64)[:, None] == 0, pos_r[None, :], pos_c[None, :]).astype(np.float32)
    ang = (pos * inv[dd % 32][:, None]).astype(np.float32)
    return np.stack([np.cos(ang), np.sin(ang)]).astype(np.float32)


def _shared_inputs(inp, layers):
    m = {}
    wa = inp["w_ada"]
    m["wada"] = np.ascontiguousarray(wa.reshape(2, KC, 128, 48, 128).transpose(0, 3, 2, 1, 4)).reshape(2, 48, 128, KC * 128)
    m["badac"] = np.ascontiguousarray(inp["b_ada"].reshape(2, 48, 128).transpose(0, 2, 1))
    m["ngc"] = np.ascontiguousarray(inp["norm_gain"].reshape(2, KC, 128).transpose(0, 2, 1))
    m["cst"], m["cstb"] = _consts()
    if 0 in layers:
        w = inp["w_in_even"][0]
        m["w0a"] = np.stack([np.stack([_lay(w, base + h * 128, 128) for base in (0, 1024, 2048, 3072, 6176)])
                             for h in range(8)])
        m["w0b"] = np.stack([np.stack([_lay(w, 4096 + g * 128, 128), _lay(w, 4608 + g * 128, 128),
                                       _lay(w, 5120 + g * 256, 128), _lay(w, 5120 + g * 256 + 128, 128),
                                       _lay(w, 7200 + g * 256, 128), _lay(w, 7200 + g * 256 + 128, 128)])
                             for g in range(4)])
        m["w0z"] = np.stack([_lay(w, 6144, 16), _lay(w, 6160, 16)])
        m["wa2"] = np.ascontiguousarray(inp["w_a2"][0].transpose(1, 0, 2))
        m["ba2c"] = np.ascontiguousarray(inp["b_a2"][0].reshape(2, 4, 128).transpose(2, 0, 1).reshape(128, 8))
        m["lbl"] = np.ascontiguousarray(inp["lb_logits"].reshape(3, 8, 128).transpose(2, 0, 1))
        m["g0c"] = np.ascontiguousarray(np.stack([inp["a_out_gain"][0], inp["b_out_gain"][0][:128],
                                                  inp["b_out_gain"][0][128:]], axis=1))
        wo = inp["w_out_even"][0]
        m["wo0"] = np.stack([_lay(wo, n * 128, 128) for n in range(16)])
    if 1 in layers:
        w = inp["w_in_odd"][0]
        m["w1"] = np.stack([np.stack([_lay(w, base + h * 256 + n * 128, 128) for base in (0, 2048, 4096, 6144)
                                      for n in range(2)]) for h in range(8)])
        m["qkg"] = np.ascontiguousarray(np.stack([inp["q_norm_gain"][0], inp["k_norm_gain"][0]], axis=1))
        m["lqb"] = np.ascontiguousarray(np.broadcast_to(inp["lambda_qk"][0][None], (128, 4, 128)))
        m["cogb"] = np.ascontiguousarray(np.broadcast_to(inp["c_out_gain"][0][None], (128, 256)))
        wo = inp["w_out_odd"][0]
        m["wo1"] = np.stack([_lay(wo, n * 128, 128) for n in range(16)])
        m["rope"] = _rope_tables()
    return {k: np.ascontiguousarray(v, dtype=np.float32) for k, v in m.items()}


def _core_inputs(inp, b):
    cc = np.stack([inp["c"][b], inp["c_ctx"]], axis=0)
    return {"cT": np.ascontiguousarray(cc.reshape(2, KC, 128).transpose(2, 1, 0), dtype=np.float32)}


def build_nc(layers=(0, 1), **kw):
    nc = bass.Bass("TRN2", target_bir_lowering=False)
    with ExitStack() as st:
        b = Builder(nc, st, layers)
        for k, v in kw.items():
            setattr(b, k, v)
        b.build()
    return nc


def run_layers(inp, layers, x1=None, ctx1=None, cores=8, **kw):
    shared = _shared_inputs(inp, layers)
    nc = build_nc(layers, **kw)
    in_maps = []
    for b in range(cores):
        m = dict(shared)
        m.update(_core_inputs(inp, b))
        if 0 in layers:
            m["x"] = np.ascontiguousarray(inp["x"][b], dtype=np.float32)
            m["ctx"] = np.ascontiguousarray(inp["ctx"][b], dtype=np.float32)
        else:
            m["x1"] = np.ascontiguousarray(x1[b], dtype=np.float32)
            m["ctx1"] = np.ascontiguousarray(ctx1[b], dtype=np.float32)
        in_maps.append(m)
    res = run_bass_kernel_spmd(nc, in_maps, core_ids=list(range(cores)))
    return res.results


def kernel(**inputs):
    inp = {k: np.asarray(v) for k, v in inputs.items()}
    r = run_layers(inp, (0, 1))
    return np.stack([x["out"] for x in r]).astype(np.float32)
```

```python
import math
import numpy as np
from contextlib import ExitStack
import concourse.bass as bass
import concourse.mybir as mybir
from concourse.bass_utils import run_bass_kernel_spmd

F32 = mybir.dt.float32
BF16 = mybir.dt.bfloat16
ALU = mybir.AluOpType
AF = mybir.ActivationFunctionType
AX = mybir.AxisListType

D = 2048
SEQ = 2048
NCTX = 256
L = SEQ + NCTX
NT = L // 128
KC = D // 128
EPS = 1e-6
LAM_INIT = 0.8 - 0.6 * math.exp(-0.3 * 1)
BLOCKS = [(0, 256), (256, 512), (768, 512), (1280, 512), (1792, 512)]


class Buf:
    __slots__ = ("name", "w", "r")

    def __init__(self, name):
        self.name = name
        self.w = None
        self.r = []


class Tile:
    def __init__(self, t, name):
        self.t = t
        self.b = Buf(name)

    def __getitem__(self, k):
        return self.t[k]


class Op:
    __slots__ = ("eng", "fn", "deps", "signal", "sem", "value", "dma", "idx")

    def __init__(self, eng, fn, dma):
        self.eng = eng
        self.fn = fn
        self.deps = set()
        self.signal = False
        self.sem = None
        self.value = 0
        self.dma = dma
        self.idx = 0


class _SemCtr:
    def __init__(self, prog, name, limit):
        self.prog = prog
        self.name = name
        self.limit = limit
        self.sem = None
        self.val = 0
        self.n = 0

    def bump(self, inc):
        if self.sem is None or self.val + inc > self.limit:
            self.sem = self.prog.new_sem(f"{self.name}_{self.n}")
            self.n += 1
            self.val = 0
        self.val += inc
        return self.sem, self.val


class Prog:
    ENGS = ("pe", "act", "dve", "pool", "sp")

    def __init__(self, nc, stack):
        self.nc = nc
        self.stack = stack
        self.ops = {e: [] for e in self.ENGS}
        self.nops = 0
        self.nsem = 0
        self.pending_dma = []
        self.last_real = {e: None for e in self.ENGS}

    def init_arena(self, words):
        self.arena = self.stack.enter_context(self.nc.sbuf_tensor("arena", [128, words], F32))
        self.arena_words = words
        self.arena_off = 0
        self.arena_n = 0

    def alloc(self, name, shape, dtype):
        per = 1
        for x in shape[1:]:
            per *= x
        words = per if dtype == F32 else (per + 1) // 2
        words = (words + 7) // 8 * 8
        assert self.arena_off + words <= self.arena_words, (name, self.arena_off, words)
        v = self.arena[0:shape[0], self.arena_off:self.arena_off + words]
        self.arena_off += words
        if dtype != F32:
            v = v.bitcast(dtype)
        v = v[:, 0:per]
        if len(shape) == 3:
            v = v.rearrange("p (a b) -> p a b", b=shape[2])
        self.arena_n += 1
        return Tile(v, f"{name}_{self.arena_n}")

    def ring(self, name, shape, dtype, n):
        return Ring([self.alloc(f"{name}{i}", shape, dtype) for i in range(n)])

    def barrier(self):
        deps = set(self.pending_dma)
        for e in self.ENGS:
            if self.last_real[e] is not None:
                deps.add(self.last_real[e])
        for e in self.ENGS:
            self.fence(e, deps)
        self.pending_dma = []

    def arena_reset(self):
        self.barrier()
        self.arena_off = 0

    def new_sem(self, name):
        self.nsem += 1
        return self.stack.enter_context(self.nc.semaphore(name))

    def sb(self, name, shape, dtype):
        return Tile(self.stack.enter_context(self.nc.sbuf_tensor("sb_" + name, list(shape), dtype)), name)

    def ps(self, name, shape, dtype=F32):
        return Tile(self.stack.enter_context(self.nc.psum_tensor("ps_" + name, list(shape), dtype)), name)

    def op(self, eng, fn, reads=(), writes=(), dma=None):
        o = Op(eng, fn, dma)
        o.idx = self.nops
        self.nops += 1
        reads = [x.b if isinstance(x, Tile) else x for x in reads]
        writes = [x.b if isinstance(x, Tile) else x for x in writes]
        for r in reads:
            if r.w is not None:
                o.deps.add(r.w)
        for w in writes:
            if w.w is not None:
                o.deps.add(w.w)
            for rr in w.r:
                o.deps.add(rr)
        for r in reads:
            r.r.append(o)
        for w in writes:
            w.w = o
            w.r = []
        if eng == "pe" and dma is None:
            o.deps = {d for d in o.deps if not (d.eng == "pe" and d.dma is None)}
        o.deps.discard(o)
        self.ops[eng].append(o)
        if dma is not None:
            self.pending_dma.append(o)
        else:
            self.last_real[eng] = o
        return o

    def fence(self, eng, ops):
        o = Op(eng, None, None)
        o.idx = self.nops
        self.nops += 1
        o.deps = set(ops)
        self.ops[eng].append(o)
        return o

    def mm(self, out, lhsT, rhs, start, stop, r, w):
        return self.op("pe", lambda e: e.matmul(out, lhsT=lhsT, rhs=rhs, start=start, stop=stop), r, w)

    def tr(self, out, in_, ident, r, w):
        return self.op("pe", lambda e: e.transpose(out, in_, ident), r, w)

    def act(self, out, in_, func, r, w, scale=1.0, bias=0.0, accum=None):
        if accum is None:
            return self.op("act", lambda e: e.activation(out=out, in_=in_, func=func, scale=scale, bias=bias), r, w)
        return self.op("act", lambda e: e.activation(out=out, in_=in_, func=func, scale=scale, bias=bias,
                                                     accum_out=accum), r, w)

    def tt(self, eng, out, in0, in1, op, r, w):
        return self.op(eng, lambda e: e.tensor_tensor(out=out, in0=in0, in1=in1, op=op), r, w)

    def ts(self, eng, out, in0, s1, s2, op0, op1, r, w):
        return self.op(eng, lambda e: e.tensor_scalar(out=out, in0=in0, scalar1=s1, scalar2=s2, op0=op0, op1=op1), r, w)

    def stt(self, out, in0, scalar, in1, op0, op1, r, w):
        return self.op("dve", lambda e: e.scalar_tensor_tensor(out=out, in0=in0, scalar=scalar, in1=in1,
                                                               op0=op0, op1=op1), r, w)

    def cp(self, eng, out, in_, r, w):
        if eng == "act":
            return self.op("act", lambda e: e.copy(out=out, in_=in_), r, w)
        return self.op(eng, lambda e: e.tensor_copy(out=out, in_=in_), r, w)

    def memset(self, eng, ap, val, w):
        return self.op(eng, lambda e: e.memset(ap, val), (), w)

    def dma(self, eng, out, in_, r, w, key):
        return self.op(eng, lambda e: e.dma_start(out=out, in_=in_), r, w, dma=key)

    def finalize(self):
        nc = self.nc
        for e in self.ENGS:
            for o in self.ops[e]:
                for d in o.deps:
                    d.signal = True
        ectr = {e: _SemCtr(self, f"s_{e}", 30000) for e in self.ENGS}
        kctr = {}
        for e in self.ENGS:
            for o in self.ops[e]:
                if o.dma is not None:
                    if o.dma not in kctr:
                        kctr[o.dma] = _SemCtr(self, f"d_{o.dma}", 30000)
                    o.sem, o.value = kctr[o.dma].bump(16)
                elif o.signal and o.fn is not None:
                    o.sem, o.value = ectr[e].bump(1)

        def run(eng_name, eng):
            known = {}
            for o in self.ops[eng_name]:
                need = {}
                for d in o.deps:
                    k = id(d.sem)
                    if k not in need or need[k][1] < d.value:
                        need[k] = (d.sem, d.value)
                for k, (sem, v) in need.items():
                    if known.get(k, 0) < v:
                        eng.wait_ge(sem, v)
                        known[k] = v
                if o.fn is None:
                    continue
                ins = o.fn(eng)
                if o.dma is not None:
                    ins.then_inc(o.sem, 16)
                elif o.signal:
                    ins.then_inc(o.sem, 1)

        with nc.Block() as block:
            @block.sync
            def _(sync):
                run("sp", sync)

            @block.tensor
            def _(tensor):
                run("pe", tensor)

            @block.scalar
            def _(scalar):
                run("act", scalar)

            @block.vector
            def _(vector):
                run("dve", vector)

            @block.gpsimd
            def _(gpsimd):
                run("pool", gpsimd)


class Ring:
    def __init__(self, tiles):
        self.tiles = tiles
        self.i = 0

    def next(self):
        t = self.tiles[self.i % len(self.tiles)]
        self.i += 1
        return t


ARENA_WORDS = 25216


class Builder:
    def __init__(self, nc, st, layers=(0, 1)):
        self.nc = nc
        self.P = P = Prog(nc, st)
        self.layers = layers
        self.out_ops = []
        dt = nc.dram_tensor

        def ein(name, shape, dtype=F32):
            return dt(name, list(shape), dtype, kind="ExternalInput").ap()

        self.d = d = {}
        d["cT"] = ein("cT", [128, KC, 2])
        d["wada"] = ein("wada", [2, 48, 128, KC * 128])
        d["badac"] = ein("badac", [2, 128, 48])
        d["ngc"] = ein("ngc", [2, 128, KC])
        d["cst"] = ein("cst", [128, 384])
        d["cstb"] = ein("cstb", [128, 1280])
        if 0 in layers:
            d["x"] = ein("x", [SEQ, D])
            d["ctx"] = ein("ctx", [NCTX, D])
            d["w0a"] = ein("w0a", [8, 5, 128, KC, 128])
            d["w0b"] = ein("w0b", [4, 6, 128, KC, 128])
            d["w0z"] = ein("w0z", [2, 128, KC, 16])
            d["wa2"] = ein("wa2", [16, 2, 512])
            d["ba2c"] = ein("ba2c", [128, 8])
            d["lbl"] = ein("lbl", [128, 3, 8])
            d["g0c"] = ein("g0c", [128, 3])
            d["wo0"] = ein("wo0", [16, 128, KC, 128])
        if 1 in layers:
            d["w1"] = ein("w1", [8, 8, 128, KC, 128])
            d["qkg"] = ein("qkg", [128, 2])
            d["lqb"] = ein("lqb", [128, 4, 128])
            d["cogb"] = ein("cogb", [128, 256])
            d["wo1"] = ein("wo1", [16, 128, KC, 128])
            d["rope"] = ein("rope", [2, 128, SEQ])
        if layers == (0, 1):
            d["x1"] = dt("x1", [SEQ, D], F32, kind="Internal").ap()
            d["ctx1"] = dt("ctx1", [NCTX, D], F32, kind="Internal").ap()
            d["out"] = dt("out", [SEQ, D], F32, kind="ExternalOutput").ap()
        elif layers == (0,):
            d["x1"] = dt("x1", [SEQ, D], F32, kind="ExternalOutput").ap()
            d["ctx1"] = dt("ctx1", [NCTX, D], F32, kind="ExternalOutput").ap()
        else:
            d["x1"] = ein("x1", [SEQ, D])
            d["ctx1"] = ein("ctx1", [NCTX, D])
            d["out"] = dt("out", [SEQ, D], F32, kind="ExternalOutput").ap()
        d["og"] = dt("og", [KC, 128, L], BF16, kind="Internal").ap()
        self.db = {k: Buf("dram_" + k) for k in ("x1", "ctx1", "og", "out")}

        self.hT = P.sb("hT", [128, KC, L], BF16)
        self.hTb = [Buf(f"hT{t}") for t in range(NT)]
        self.hTc = [Buf(f"hTc{t}") for t in range(NT)]
        self.cst = P.sb("cst", [128, 384], F32)
        self.cstb = P.sb("cstb", [128, 1280], BF16)
        self.small = P.sb("small", [128, 8], F32)
        self.modc = P.sb("modc", [128, 48, 2], F32)
        self.Acol = P.sb("Acol", [128, KC, 2], F32)
        self.ngc = P.sb("ngc", [128, KC], F32)
        self.badac = P.sb("badac", [128, 48], F32)
        self.scT = P.sb("scT", [128, KC, 2], F32)
        self.bank = [P.ps(f"bank{i}", [128, 512]) for i in range(8)]
        self.wfree_list = [P.sb(f"wr{i}", [128, KC, 128], BF16) for i in range(8)]
        P.init_arena(ARENA_WORDS)

        P.dma("sp", self.cst[:], d["cst"], [], [self.cst], "cst")
        P.dma("pool", self.cstb[:], d["cstb"], [], [self.cstb], "cstb")
        P.memset("dve", self.small[:, 0:1], EPS, [self.small])
        P.memset("dve", self.small[:, 1:2], 1.0, [self.small])
        P.memset("dve", self.small[:, 2:3], EPS / (1.0 - LAM_INIT) ** 2, [self.small])
        self.ident = self.cst[:, 0:128]
        self.mask = {"f": self.cst[:, 128:256], "b": self.cst[:, 256:384]}
        self.cm4b = self.cstb[:, 0:512]
        self.rotTb = self.cstb[:, 512:640]
        self.onesb = self.cstb[:, 640:768]
        self.mask64 = {"f": self.cstb[:, 768:896], "b": self.cstb[:, 896:1024]}
        self.cm2b = self.cstb[:, 1024:1280]
        self.eps = self.small[:, 0:1]
        self.one = self.small[:, 1:2]

    def wload(self, src):
        t = self.wfree_list.pop(0)
        ncols = src.shape[-1]
        self.P.dma("pool", t[:, :, 0:ncols], src, [], [t], "w_" + t.b.name)
        return t

    def wfree(self, *ts):
        for t in ts:
            if isinstance(t, (list, tuple)):
                self.wfree(*t)
            elif t is not None:
                assert t not in self.wfree_list
                self.wfree_list.append(t)

    def adaln_gen(self, l):
        P, d = self.P, self.d
        P.dma("sp", self.ngc[:], d["ngc"][l], [], [self.ngc], "ngc")
        P.dma("sp", self.badac[:], d["badac"][l], [], [self.badac], "badac")
        if l == self.layers[0]:
            P.dma("sp", self.scT[:], d["cT"], [], [self.scT], "scT")
            P.act(self.scT[:], self.scT[:], AF.Silu, [self.scT], [self.scT])
        row = P.alloc("adrow", [2, 128], F32)
        big = P.ring("wadab", [128, KC * 128], F32, 3)
        pb = Ring([self.bank[4], self.bank[5]])
        wts = {}

        def issue(n):
            wt = big.next()
            P.dma("act", wt[:], d["wada"][l, n], [], [wt], "big_" + wt.b.name)
            wts[n] = wt

        issue(0)
        issue(1)
        for n in range(48):
            if n + 2 < 48:
                issue(n + 2)
            wt = wts.pop(n)
            wv = wt[:].rearrange("p (k j) -> p k j", j=128)
            ps = pb.next()
            for k in range(KC):
                P.mm(ps[0:2, 0:128], self.scT[:, k, :], wv[:, k, :], k == 0, k == KC - 1, [self.scT, wt], [ps])
            P.cp("act", row[:], ps[0:2, 0:128], [ps], [row])
            ps2 = pb.next()
            P.tr(ps2[:, 0:2], row[:], self.ident[0:2, 0:2], [row, self.cst], [ps2])
            P.ts("dve", self.modc[:, n, :], ps2[:, 0:2], self.badac[:, n:n + 1], None, ALU.add, ALU.bypass,
                 [ps2, self.badac], [self.modc])
            yield
        for r in range(2):
            P.stt(self.Acol[:, :, r], self.modc[:, 16:32, r], 1.0, self.ngc[:], ALU.add, ALU.mult,
                  [self.modc, self.ngc], [self.Acol])

    def adaln(self, l):
        for _ in self.adaln_gen(l):
            pass

    def build_hT(self, xsrc, csrc, src_bufs):
        P = self.P
        xn = P.ring("xn", [128, D], F32, 2)
        junk = P.alloc("junk", [128, D], BF16)
        str_ = P.ring("nstat", [128, 4], F32, 2)
        big = P.ring("xin", [128, D], F32, 2)
        pb = Ring([self.bank[0], self.bank[1], self.bank[2], self.bank[3]])
        for t in range(NT):
            r = 1 if t < 2 else 0
            src = csrc[t * 128:(t + 1) * 128, :] if t < 2 else xsrc[(t - 2) * 128:(t - 1) * 128, :]
            xt = big.next()
            st = str_.next()
            xnt = xn.next()
            P.dma("sp", xt[:], src, src_bufs, [xt], "big_" + xt.b.name)
            P.act(junk[:], xt[:], AF.Square, [xt], [junk, st], accum=st[:, 0:1])
            P.act(st[:, 1:2], st[:, 0:1], AF.Sqrt, [st, self.small], [st], scale=1.0 / D, bias=self.eps)
            P.op("dve", lambda e, st=st: e.reciprocal(st[:, 2:3], st[:, 1:2]), [st], [st])
            P.ts("dve", xnt[:], xt[:], st[:, 2:3], None, ALU.mult, ALU.bypass, [xt, st], [xnt])
            for g in range(4):
                ps = pb.next()
                for j in range(4):
                    k = 4 * g + j
                    P.tr(ps[:, 128 * j:128 * j + 128], xnt[:, 128 * k:128 * k + 128], self.ident, [xnt, self.cst], [ps])
                for j in range(4):
                    k = 4 * g + j
                    if g % 2 == 0:
                        P.act(self.hT[:, k, t * 128:(t + 1) * 128], ps[:, 128 * j:128 * j + 128], AF.Identity,
                              [ps, self.Acol, self.modc], [self.hTb[t]], scale=self.Acol[:, k, r:r + 1],
                              bias=self.modc[:, k, r:r + 1])
                    else:
                        P.ts("dve", self.hT[:, k, t * 128:(t + 1) * 128], ps[:, 128 * j:128 * j + 128],
                             self.Acol[:, k, r:r + 1], self.modc[:, k, r:r + 1], ALU.mult, ALU.add,
                             [ps, self.Acol, self.modc], [self.hTc[t]])

    def hr(self, s0, bw):
        return self.hTb[s0 // 128:(s0 + bw) // 128] + self.hTc[s0 // 128:(s0 + bw) // 128]

    def out_proj(self, wsrc, xsrc, csrc, src_bufs, xdst, cdst, dst_bufs, tok0, is_out, side=None):
        P, d = self.P, self.d
        G = self.hT
        Gb = [Buf(f"Gt{t}") for t in range(NT)]
        for bi, (s0, bw) in enumerate(BLOCKS):
            if s0 < tok0:
                continue
            P.dma("sp" if bi % 2 == 0 else "act", G[:, :, s0:s0 + bw], d["og"][:, :, s0:s0 + bw].rearrange("k p t -> p k t"),
                  [self.db["og"]], Gb[s0 // 128:(s0 + bw) // 128], f"G{bi}")
        pb = Ring([self.bank[0], self.bank[1], self.bank[2], self.bank[3]])
        nr = 2 if tok0 == 0 else 1
        self.gbc = [P.alloc(f"gbc{r}", [128, D], F32) for r in range(nr)]
        gtmp = P.ring("gtmp", [128, 128], F32, 2)
        for r in range(nr):
            for k in range(KC):
                tmp = gtmp.next()
                P.cp("dve", tmp[:], self.modc[:, 32 + k, r:r + 1].to_broadcast([128, 128]), [self.modc], [tmp])
                ps = pb.next()
                P.tr(ps[:, 0:128], tmp[:], self.ident, [tmp, self.cst], [ps])
                P.cp("act", self.gbc[r][:, 128 * k:128 * k + 128], ps[:, 0:128], [ps], [self.gbc[r]])
        xr = P.ring("xo", [128, 512], F32, 3)
        tr_ = P.ring("xt", [128, 512], F32, 3)
        wts_next = [self.wload(wsrc[j]) for j in range(4)]
        for n in range(4):
            wts = wts_next
            if n + 1 < 4:
                wts_next = [self.wload(wsrc[4 * (n + 1) + j]) for j in range(4)]
            for t in range(tok0 // 128, NT):
                r = 1 if t < 2 else 0
                if t < 2:
                    src = csrc[t * 128:(t + 1) * 128, n * 512:(n + 1) * 512]
                    dst = cdst[t * 128:(t + 1) * 128, n * 512:(n + 1) * 512]
                else:
                    src = xsrc[(t - 2) * 128:(t - 1) * 128, n * 512:(n + 1) * 512]
                    dst = xdst[(t - 2) * 128:(t - 1) * 128, n * 512:(n + 1) * 512]
                xt = xr.next()
                P.dma("sp", xt[:], src, src_bufs, [xt], "xo_" + xt.b.name)
                ps = pb.next()
                for j, wt in enumerate(wts):
                    for k in range(KC):
                        P.mm(ps[:, 128 * j:128 * j + 128], G[:, k, t * 128:(t + 1) * 128], wt[:, k, :],
                             k == 0, k == KC - 1, [Gb[t], wt], [ps])
                tmp = tr_.next()
                P.tt("dve", tmp[:], ps[:], self.gbc[r][:, n * 512:(n + 1) * 512], ALU.mult, [ps, self.gbc[r]], [tmp])
                P.tt("pool", tmp[:], tmp[:], xt[:], ALU.add, [tmp, xt], [tmp])
                o = P.dma("pool", dst, tmp[:], [tmp], dst_bufs, "xs_" + tmp.b.name)
                if is_out:
                    self.out_ops.append(o)
                if side is not None:
                    try:
                        next(side)
                    except StopIteration:
                        side = None
            self.wfree(wts)
        if side is not None:
            for _ in side:
                pass

    def layer0(self):
        P, d = self.P, self.d
        self.adaln(0)
        P.arena_reset()
        self.build_hT(d["x"], d["ctx"], [])
        P.arena_reset()
        hT = self.hT
        lbl = P.alloc("lbl", [128, 3, 8], F32)
        lbt = P.alloc("lbt", [128, 4, 8], F32)
        P.dma("sp", lbl[:], d["lbl"], [], [lbl], "lbl")
        P.act(lbl[:], lbl[:], AF.Exp, [lbl], [lbl])
        P.tt("dve", lbt[:, 3, :], lbl[:, 0, :], lbl[:, 1, :], ALU.add, [lbl], [lbt])
        P.tt("dve", lbt[:, 3, :], lbt[:, 3, :], lbl[:, 2, :], ALU.add, [lbl, lbt], [lbt])
        P.op("dve", lambda e: e.reciprocal(lbt[:, 3, :], lbt[:, 3, :]), [lbt], [lbt])
        P.tt("dve", lbt[:, 0, :], lbl[:, 0, :], lbt[:, 3, :], ALU.mult, [lbl, lbt], [lbt])
        P.ts("dve", lbt[:, 1, :], lbt[:, 0, :], -1.0, 1.0, ALU.mult, ALU.add, [lbt], [lbt])
        P.ts("dve", lbt[:, 2, :], lbt[:, 1, :], -1.0, None, ALU.mult, ALU.bypass, [lbt], [lbt])
        g0c = P.alloc("g0c", [128, 3], F32)
        P.dma("sp", g0c[:], d["g0c"], [], [g0c], "g0c")
        ba2 = P.alloc("ba2", [128, 8], F32)
        P.dma("sp", ba2[:], d["ba2c"], [], [ba2], "ba2")
        P.ts("dve", ba2[:], ba2[:], -1.0, None, ALU.mult, ALU.bypass, [ba2], [ba2])
        wa2 = P.alloc("wa2", [16, 2, 128], F32)
        wz = [P.alloc(f"wz{i}", [128, KC, 16], BF16) for i in range(2)]
        for i in range(2):
            P.dma("pool", wz[i][:], d["w0z"][i], [], [wz[i]], f"wz{i}")

        qs = P.alloc("qs", [128, L], BF16)
        kk = P.alloc("kk", [128, L], BF16)
        sg = [P.alloc("sg0", [128, L], BF16), P.alloc("sg1", [128, L], BF16)]
        vv = P.alloc("vv", [128, NT, 256], BF16)
        vvb = [Buf(f"vv{t}") for t in range(NT)]
        oacc = [P.alloc("oacc0", [128, L], F32), P.alloc("oacc1", [128, L], F32)]
        oab = [[Buf(f"oa{vc}_{t}") for t in range(NT)] for vc in range(2)]
        pb = Ring([self.bank[0], self.bank[1]])
        ftmp = P.ring("ftmp", [128, 520], F32, 5)
        btmp = P.ring("btmp", [128, 512], BF16, 3)
        zs = P.ring("zs", [16, 512], F32, 1)

        def proj_fm(wap, ncol, blk, rd):
            s0, bw = BLOCKS[blk]
            ps = pb.next()
            for k in range(KC):
                P.mm(ps[0:ncol, 0:bw], wap[:, k, 0:ncol], hT[:, k, s0:s0 + bw], k == 0, k == KC - 1,
                     rd + self.hr(s0, bw), [ps])
            return ps

        W = {}
        for dr in "fb":
            W[dr] = dict(
                qd=P.ring(f"qd{dr}", [128, 512], BF16, 2), ki=P.ring(f"ki{dr}", [128, 512], BF16, 2),
                ktT=P.ring(f"ktT{dr}", [128, 512], F32, 2), gt=P.ring(f"gt{dr}", [128, 16], F32, 2),
                kt=P.ring(f"kt{dr}", [128, 128], BF16, 2), ktm=P.ring(f"ktm{dr}", [128, 512], BF16, 3),
                scm=P.ring(f"scm{dr}", [128, 128], BF16, 2),
                S32=P.ring(f"S32{dr}", [128, 256], F32, 3), Sbf=P.ring(f"Sbf{dr}", [128, 256], BF16, 5),
            )
        PB = {"f": (self.bank[2], self.bank[3], self.bank[4]), "b": (self.bank[5], self.bank[6], self.bank[7])}

        def elementwise(dr, blk, is_a, wdir, hd):
            s0, bw = BLOCKS[blk]
            CS = 32 if is_a else 64
            ncb = bw // CS
            w = W[dr]
            f1, f2, Fp, f3 = ftmp.next(), ftmp.next(), ftmp.next(), ftmp.next()
            qd, ki, ktT, gt = w["qd"].next(), w["ki"].next(), w["ktT"].next(), w["gt"].next()
            if is_a:
                ls = 1.0
                ps = proj_fm(wdir, 128, blk, [wdir])
                P.act(f1[:, 0:bw], ps[:, 0:bw], AF.Sigmoid, [ps], [f1])
                kb = btmp.next()
                P.ts("pool", kb[:, 0:bw], f1[:, 0:bw], lbt[:, 2, hd:hd + 1], lbt[:, 1, hd:hd + 1], ALU.mult, ALU.add,
                     [f1, lbt], [kb])
                ksrc, kbuf = kb[:, 0:bw], kb
                P.act(f2[:, 0:bw], f1[:, 0:bw], AF.Ln, [f1, lbt], [f2], scale=lbt[:, 1, hd:hd + 1],
                      bias=lbt[:, 0, hd:hd + 1])
            else:
                ls = -1.0 / 16.0
                di = 0 if dr == "f" else 1
                ps = proj_fm(wz[di], 16, blk, [wz[di]])
                z = zs.next()
                P.cp("act", z[:, 0:bw], ps[0:16, 0:bw], [ps], [z])
                ps = pb.next()
                P.mm(ps[:, 0:bw], wa2[:, di, :], z[:, 0:bw], True, True, [wa2, z], [ps])
                P.act(f1[:, 0:bw], ps[:, 0:bw], AF.Exp, [ps, ba2], [f1], scale=-1.0,
                      bias=ba2[:, di * 4 + hd:di * 4 + hd + 1])
                P.act(f2[:, 0:bw], f1[:, 0:bw], AF.Ln, [f1, self.small], [f2], scale=1.0, bias=self.one)
                ksrc, kbuf = kk[:, s0:s0 + bw], kk
            P.memset("dve", Fp[:, 0:1], 0.0, [Fp])
            P.op("dve", lambda e: e.tensor_tensor_scan(out=Fp[:, 1:1 + bw], data0=f2[:, 0:bw], data1=f2[:, 0:bw],
                                                       initial=0.0, op0=ALU.add, op1=ALU.bypass), [f2], [Fp])
            c3 = lambda ap: ap.rearrange("p (c j) -> p c j", j=CS)
            Fin, Fex = c3(Fp[:, 1:1 + bw]), c3(Fp[:, 0:bw])
            Rb = Fex[:, :, 0:1].to_broadcast([128, ncb, CS])
            Eb = Fin[:, :, CS - 1:CS].to_broadcast([128, ncb, CS])
            if dr == "f":
                P.tt("dve", c3(f1[:, 0:bw]), Fin, Rb, ALU.subtract, [Fp], [f1])
                P.tt("pool", c3(f3[:, 0:bw]), Eb, Fin, ALU.subtract, [Fp], [f3])
            else:
                P.tt("dve", c3(f1[:, 0:bw]), Eb, Fex, ALU.subtract, [Fp], [f1])
                P.tt("pool", c3(f3[:, 0:bw]), Fex, Rb, ALU.subtract, [Fp], [f3])
            P.tt("dve", gt[:, 0:ncb], Fin[:, :, CS - 1], Fex[:, :, 0], ALU.subtract, [Fp], [gt])
            P.act(gt[:, 0:ncb], gt[:, 0:ncb], AF.Exp, [gt], [gt], scale=ls)
            P.act(f2[:, 0:bw], f1[:, 0:bw], AF.Exp, [f1], [f2], scale=-ls)
            P.act(f1[:, 0:bw], f1[:, 0:bw], AF.Exp, [f1], [f1], scale=ls)
            P.act(f3[:, 0:bw], f3[:, 0:bw], AF.Exp, [f3], [f3], scale=ls)
            P.tt("dve", qd[:, 0:bw], qs[:, s0:s0 + bw], f1[:, 0:bw], ALU.mult, [qs, f1], [qd])
            P.tt("pool", ki[:, 0:bw], ksrc, f2[:, 0:bw], ALU.mult, [kbuf, f2], [ki])
            P.tt("pool", ktT[:, 0:bw], ksrc, f3[:, 0:bw], ALU.mult, [kbuf, f3], [ktT])
            return qd, ki, ktT, gt

        def chain(dr, is_a, wdir, hd, written):
            w = W[dr]
            nvc = 1 if is_a else 2
            dv = 128 * nvc
            po_b, U_b, sc_b = PB[dr]
            S32 = w["S32"].next()
            Sb = w["Sbf"].next()
            P.memset("pool", S32[:, 0:dv], 0.0, [S32])
            P.memset("pool", Sb[:, 0:dv], 0.0, [Sb])
            blocks = [0, 1, 2, 3, 4] if dr == "f" else [0, 4, 3, 2, 1]
            CS = 32 if is_a else 64
            NS = 128 // CS
            corder = list(range(NS)) if dr == "f" else list(range(NS))[::-1]
            cmb = self.cm4b if is_a else self.cm2b
            msk = self.mask[dr] if is_a else self.mask64[dr]
            ew_next = elementwise(dr, blocks[0], is_a, wdir, hd)
            for bi, blk in enumerate(blocks):
                s0, bw = BLOCKS[blk]
                qd, ki, ktT, gt = ew_next
                yield
                tl = list(range(bw // 128))
                if dr == "b":
                    tl = tl[::-1]

                def prep(ti):
                    P.tr(sc_b[:, 128:256], ktT[:, ti * 128:ti * 128 + 128], self.ident, [ktT, self.cst], [sc_b])
                    kt = w["kt"].next()
                    P.cp("act", kt[:], sc_b[:, 128:256], [sc_b], [kt])
                    ktm_ = w["ktm"].next()
                    P.tt("pool", ktm_[:, 0:NS * 128].rearrange("p (c d) -> p c d", d=128),
                         kt[:].rearrange("p (o d) -> p o d", o=1).to_broadcast([128, NS, 128]),
                         cmb.rearrange("p (c d) -> p c d", d=128), ALU.mult, [kt, self.cstb], [ktm_])
                    return ktm_

                ktm_next = prep(tl[0])
                for idx, ti in enumerate(tl):
                    tg = s0 // 128 + ti
                    tsl = slice(ti * 128, ti * 128 + 128)
                    ktm = ktm_next
                    P.mm(sc_b[:, 0:128], ki[:, tsl], qd[:, tsl], True, True, [ki, qd], [sc_b])
                    scm = w["scm"].next()
                    P.tt("dve", scm[:], sc_b[:, 0:128], msk, ALU.mult, [sc_b, self.cst, self.cstb], [scm])
                    Ss = [Sb]
                    for half in range(1):
                        cs = corder
                        for i, c in enumerate(cs):
                            P.mm(U_b[:, i * dv:(i + 1) * dv], ktm[:, c * 128:(c + 1) * 128], vv[:, tg, 0:dv],
                                 True, True, [ktm, vvb[tg]], [U_b])
                        for i, c in enumerate(cs):
                            gcol = gt[:, ti * NS + c:ti * NS + c + 1]
                            Sn = w["S32"].next()
                            P.stt(Sn[:, 0:dv], S32[:, 0:dv], gcol, U_b[:, i * dv:(i + 1) * dv], ALU.mult, ALU.add,
                                  [S32, gt, U_b], [Sn])
                            S32 = Sn
                            Sb = w["Sbf"].next()
                            P.cp("act", Sb[:, 0:dv], S32[:, 0:dv], [S32], [Sb])
                            Ss.append(Sb)
                    if idx + 1 < len(tl):
                        ktm_next = prep(tl[idx + 1])
                    if idx == 0 and bi + 1 < len(blocks):
                        ew_next = elementwise(dr, blocks[bi + 1], is_a, wdir, hd)
                    yield
                    for vc in range(nvc):
                        pos = po_b[:, vc * 128:(vc + 1) * 128]
                        P.mm(pos, vv[:, tg, vc * 128:(vc + 1) * 128], scm[:], True, False, [vvb[tg], scm], [po_b])
                        for i, c in enumerate(corder):
                            P.mm(po_b[:, vc * 128 + CS * c:vc * 128 + CS * c + CS], Ss[i][:, vc * 128:(vc + 1) * 128],
                                 qd[:, ti * 128 + CS * c:ti * 128 + CS * c + CS], False, i == NS - 1, [Ss[i], qd],
                                 [po_b])
                        dst = oacc[vc][:, tg * 128:(tg + 1) * 128]
                        if (vc, tg) not in written:
                            written.add((vc, tg))
                            P.cp("act", dst, pos, [po_b], [oab[vc][tg]])
                        else:
                            P.tt("dve", dst, pos, dst, ALU.add, [po_b, oab[vc][tg]], [oab[vc][tg]])
                    yield

        def finish_head(nvc, gcol0, mix0):
            dv = 128 * nvc
            for blk in range(5):
                s0, bw = BLOCKS[blk]
                ob = lambda vc: oab[vc][s0 // 128:(s0 + bw) // 128]
                ps = pb.next()
                for vc in range(nvc):
                    sq = btmp.next()
                    P.act(sq[:, 0:bw], oacc[vc][:, s0:s0 + bw], AF.Square, ob(vc), [sq])
                    P.mm(ps[:, 0:bw], self.onesb, sq[:, 0:bw], vc == 0, vc == nvc - 1, [self.cstb, sq], [ps])
                rs = ftmp.next()
                P.act(rs[:, 0:bw], ps[:, 0:bw], AF.Ln, [ps, self.small], [rs], scale=1.0 / dv, bias=self.eps)
                P.act(rs[:, 0:bw], rs[:, 0:bw], AF.Exp, [rs], [rs], scale=-0.5)
                for vc in range(nvc):
                    t1 = ftmp.next()
                    P.stt(t1[:, 0:bw], oacc[vc][:, s0:s0 + bw], g0c[:, gcol0 + vc:gcol0 + vc + 1], rs[:, 0:bw],
                          ALU.mult, ALU.mult, ob(vc) + [g0c, rs], [t1])
                    og = btmp.next()
                    P.tt("pool", og[:, 0:bw], t1[:, 0:bw], sg[vc][:, s0:s0 + bw], ALU.mult, [t1, sg[vc]], [og])
                    P.dma("sp", d["og"][mix0 + vc, :, s0:s0 + bw], og[:, 0:bw], [og], [self.db["og"]],
                          "ogs_" + og.b.name)

        def load_head(is_a, hd):
            if is_a:
                wq, wf, wb = (self.wload(d["w0a"][hd, i]) for i in range(3))
                wv = [self.wload(d["w0a"][hd, 3])]
                wg = [self.wload(d["w0a"][hd, 4])]
                return wq, None, wf, wb, wv, wg
            wq = self.wload(d["w0b"][hd, 0])
            wk = self.wload(d["w0b"][hd, 1])
            wv = [self.wload(d["w0b"][hd, 2]), self.wload(d["w0b"][hd, 3])]
            wg = [self.wload(d["w0b"][hd, 4]), self.wload(d["w0b"][hd, 5])]
            return wq, wk, None, None, wv, wg

        def run_head(is_a, hd, wts, nxt):
            nvc = 1 if is_a else 2
            dv = 128 * nvc
            wq, wk, wf, wb, wv, wg = wts
            if not is_a:
                P.dma("sp", wa2[:], d["wa2"][:, :, hd * 128:(hd + 1) * 128], [], [wa2], "wa2")
            for blk in range(5):
                s0, bw = BLOCKS[blk]
                ps = proj_fm(wq, 128, blk, [wq])
                if is_a:
                    P.act(qs[:, s0:s0 + bw], ps[:, 0:bw], AF.Silu, [ps], [qs])
                else:
                    P.act(qs[:, s0:s0 + bw], ps[:, 0:bw], AF.Identity, [ps], [qs], scale=128.0 ** -0.5)
                    ps = proj_fm(wk, 128, blk, [wk])
                    P.cp("act", kk[:, s0:s0 + bw], ps[:, 0:bw], [ps], [kk])
                for vc in range(nvc):
                    ps = proj_fm(wg[vc], 128, blk, [wg[vc]])
                    P.act(sg[vc][:, s0:s0 + bw], ps[:, 0:bw], AF.Silu, [ps], [sg[vc]])
            written = set()
            gens = [chain("f", is_a, wf, hd, written), chain("b", is_a, wb, hd, written)]
            for g in gens:
                next(g)
            for tg in range(NT):
                ps = pb.next()
                for vc in range(nvc):
                    for k in range(KC):
                        P.mm(ps[:, vc * 128:(vc + 1) * 128], hT[:, k, tg * 128:(tg + 1) * 128], wv[vc][:, k, :],
                             k == 0, k == KC - 1, [self.hTb[tg], self.hTc[tg], wv[vc]], [ps])
                P.cp("act", vv[:, tg, 0:dv], ps[:, 0:dv], [ps], [vvb[tg]])
            self.wfree(wq, wk, wv, wg)
            nwts = load_head(*nxt) if nxt is not None else None
            while gens:
                for g in list(gens):
                    try:
                        next(g)
                    except StopIteration:
                        gens.remove(g)
            self.wfree(wf, wb)
            if is_a:
                finish_head(1, 0, hd)
            else:
                finish_head(2, 1, 8 + 2 * hd)
            return nwts

        heads = [(True, h) for h in self.heads_a] + [(False, h) for h in self.heads_b]
        wts = load_head(*heads[0])
        for i, (is_a, hd) in enumerate(heads):
            wts = run_head(is_a, hd, wts, heads[i + 1] if i + 1 < len(heads) else None)
        P.arena_reset()
        side = self.adaln_gen(1) if 1 in self.layers else None
        self.out_proj(d["wo0"], d["x"], d["ctx"], [], d["x1"], d["ctx1"], [self.db["x1"], self.db["ctx1"]], 0,
                      self.layers == (0,), side=side)
        P.arena_reset()

    heads_a = range(8)
    heads_b = range(4)

    def layer1(self):
        P, d = self.P, self.d
        srcb = [self.db["x1"], self.db["ctx1"]]
        if 0 not in self.layers:
            self.adaln(1)
            P.arena_reset()
        self.build_hT(d["x1"], d["ctx1"], srcb)
        P.arena_reset()
        hT = self.hT
        qkg = P.alloc("qkg", [128, 2], F32)
        P.dma("sp", qkg[:], d["qkg"], [], [qkg], "qkg")
        cogb = P.alloc("cogb", [128, 256], F32)
        P.dma("sp", cogb[:], d["cogb"], [], [cogb], "cogb")
        lq = P.alloc("lq", [128, 4, 128], F32)
        lam = P.alloc("lam", [128, 4], F32)
        P.dma("sp", lq[:], d["lqb"], [], [lq], "lq")
        P.tt("dve", lq[:, 0, :], lq[:, 0, :], lq[:, 1, :], ALU.mult, [lq], [lq])
        P.tt("dve", lq[:, 2, :], lq[:, 2, :], lq[:, 3, :], ALU.mult, [lq], [lq])
        P.op("dve", lambda e: e.reduce_sum(lam[:, 0:1], lq[:, 0, :], axis=AX.X), [lq], [lam])
        P.op("dve", lambda e: e.reduce_sum(lam[:, 1:2], lq[:, 2, :], axis=AX.X), [lq], [lam])
        P.act(lam[:, 0:2], lam[:, 0:2], AF.Exp, [lam], [lam])
        P.tt("dve", lam[:, 2:3], lam[:, 0:1], lam[:, 1:2], ALU.subtract, [lam], [lam])
        P.ts("dve", lam[:, 2:3], lam[:, 2:3], LAM_INIT, None, ALU.add, ALU.bypass, [lam], [lam])
        rope = P.alloc("rope", [128, 2, SEQ], F32)
        P.dma("sp", rope[:, 0, :], d["rope"][0], [], [rope], "rope")
        P.dma("sp", rope[:, 1, :], d["rope"][1], [], [rope], "rope")

        qT = [P.alloc(f"qT{i}", [128, SEQ], BF16) for i in range(2)]
        kT = [P.alloc(f"kT{i}", [128, L], BF16) for i in range(2)]
        sgT = [P.alloc(f"sgT{i}", [128, SEQ], BF16) for i in range(2)]
        va = P.alloc("va", [128, NT, 258], BF16)
        vab = [Buf(f"va{t}") for t in range(NT)]
        P.memset("pool", va[:, :, 256:257], 1.0, vab)
        ogh = P.alloc("ogh", [128, 2, SEQ], BF16)
        pb = Ring([self.bank[0], self.bank[1]])
        sq_r = P.ring("sq", [128, 512], BF16, 2)
        rs_r = P.ring("rs", [128, 512], F32, 2)
        qn_r = P.ring("qn", [128, 512], BF16, 2)
        t1_r = P.ring("t1", [128, 512], F32, 2)
        t2_r = P.ring("t2", [128, 512], F32, 2)
        po = [[self.bank[4], self.bank[5]], [self.bank[6], self.bank[7]]]
        on_r = P.ring("on", [128, 256], F32, 2)
        ot_r = P.ring("ot", [128, 256], F32, 2)
        st_r = P.ring("ast", [128, 8], F32, 2)
        junk = P.alloc("junk1", [128, 256], BF16)
        mhalf = P.alloc("mhalf", [128, 2], F32)
        P.memset("pool", mhalf[:], -0.5, [mhalf])

        slots = [(self.bank[0], self.bank[1]), (self.bank[2], self.bank[3]), (self.bank[4], self.bank[5])]
        vbank = Ring([self.bank[6], self.bank[7]])
        sT_t = [self.bank[1], self.bank[2], self.bank[3]]
        pbe = Ring([self.bank[0]])
        eT_r = P.ring("eT4", [128, 512], BF16, 4)
        pc_r = P.ring("pcopy", [128, 4, 257], F32, 2)

        def proj16(ps, wt, s0, bw):
            for k in range(KC):
                P.mm(ps[:, 0:bw], wt[:, k, :], hT[:, k, s0:s0 + bw], k == 0, k == KC - 1, [wt] + self.hr(s0, bw), [ps])

        def task_qk(slot, wt, s0, bw, gcol, dst, dcol, rope_t0):
            psA, psB = slots[slot]
            proj16(psA, wt, s0, bw)
            sq = sq_r.next()
            P.act(sq[:, 0:bw], psA[:, 0:bw], AF.Square, [psA], [sq])
            yield
            P.mm(psB[:, 0:bw], self.onesb, sq[:, 0:bw], True, True, [self.cstb, sq], [psB])
            rs = rs_r.next()
            P.act(rs[:, 0:bw], psB[:, 0:bw], AF.Ln, [psB, self.small], [rs], scale=1.0 / 128, bias=self.eps)
            P.act(rs[:, 0:bw], rs[:, 0:bw], AF.Exp, [rs], [rs], scale=-0.5)
            if rope_t0 is None:
                P.stt(dst[:, dcol:dcol + bw], psA[:, 0:bw], gcol, rs[:, 0:bw], ALU.mult, ALU.mult, [psA, qkg, rs], [dst])
                return
            qn = qn_r.next()
            P.stt(qn[:, 0:bw], psA[:, 0:bw], gcol, rs[:, 0:bw], ALU.mult, ALU.mult, [psA, qkg, rs], [qn])
            yield
            P.mm(psB[:, 0:bw], self.rotTb, qn[:, 0:bw], True, True, [self.cstb, qn], [psB])
            t1, t2 = t1_r.next(), t2_r.next()
            P.tt("pool", t1[:, 0:bw], qn[:, 0:bw], rope[:, 0, rope_t0:rope_t0 + bw], ALU.mult, [qn, rope], [t1])
            P.tt("dve", t2[:, 0:bw], psB[:, 0:bw], rope[:, 1, rope_t0:rope_t0 + bw], ALU.mult, [psB, rope], [t2])
            P.tt("pool", dst[:, dcol:dcol + bw], t1[:, 0:bw], t2[:, 0:bw], ALU.add, [t1, t2], [dst])

        def task_g(slot, wt, s0, bw, dst):
            psA, psB = slots[slot]
            proj16(psA, wt, s0, bw)
            P.act(dst[:, s0 - NCTX:s0 - NCTX + bw], psA[:, 0:bw], AF.Silu, [psA], [dst])
            return
            yield

        def task_v(slot, wv, tg):
            ps = vbank.next()
            for vc in range(2):
                for k in range(KC):
                    P.mm(ps[:, vc * 128:(vc + 1) * 128], hT[:, k, tg * 128:(tg + 1) * 128], wv[vc][:, k, :],
                         k == 0, k == KC - 1, [self.hTb[tg], self.hTc[tg], wv[vc]], [ps])
            P.cp("act" if ps is self.bank[6] else "dve", va[:, tg, 0:256], ps[:, 0:256], [ps], [vab[tg]])
            return
            yield

        def run_tasks(tasks, nslots):
            pending = list(tasks)
            active = {}
            while pending or active:
                for sl in range(nslots):
                    if sl not in active and pending:
                        active[sl] = pending.pop(0)(sl)
                    if sl in active:
                        try:
                            next(active[sl])
                        except StopIteration:
                            del active[sl]

        hc = list(self.heads_c)
        w1n = [self.wload(d["w1"][hc[0], i]) for i in (2, 0, 6, 4, 5, 3, 1, 7)]
        for hi, hd in enumerate(hc):
            wk0, wq0, wg0, wv0, wv1, wk1, wq1, wg1 = w1n
            wq, wk, wv, wg = [wq0, wq1], [wk0, wk1], [wv0, wv1], [wg0, wg1]
            tasks = []
            vt = list(range(NT))
            for n in range(2):
                for blk in range(5):
                    s0, bw = BLOCKS[blk]
                    tasks.append(lambda sl, n=n, s0=s0, bw=bw, blk=blk: task_qk(
                        sl, wk[n], s0, bw, qkg[:, 1:2], kT[n], s0, None if blk == 0 else s0 - NCTX))
                    if blk > 0:
                        tasks.append(lambda sl, n=n, s0=s0, bw=bw: task_qk(
                            sl, wq[n], s0, bw, qkg[:, 0:1], qT[n], s0 - NCTX, s0 - NCTX))
                        tasks.append(lambda sl, n=n, s0=s0, bw=bw: task_g(sl, wg[n], s0, bw, sgT[n]))
                    for _ in range(2):
                        if vt:
                            tg = vt.pop(0)
                            tasks.append(lambda sl, tg=tg: task_v(sl, wv, tg))
            while vt:
                tg = vt.pop(0)
                tasks.append(lambda sl, tg=tg: task_v(sl, wv, tg))
            run_tasks(tasks, 3)
            self.wfree(w1n)
            if hi + 1 < len(hc):
                w1n = [self.wload(d["w1"][hc[hi + 1], i]) for i in (2, 0, 6, 4, 5, 3, 1, 7)]
            LOOK = 2
            steps = [(qb, kt) for qb in range(SEQ // 256) for kt in range(NT)]
            eTs = {}
            deferred = []
            for i in range(len(steps) + LOOK):
                while deferred and deferred[0][0] <= i:
                    deferred.pop(0)[1]()
                if i < len(steps):
                    qb, kt = steps[i]
                    q0 = qb * 256
                    sT = sT_t[i % 3]
                    for n in range(2):
                        P.mm(sT[:, n * 256:(n + 1) * 256], kT[n][:, kt * 128:(kt + 1) * 128], qT[n][:, q0:q0 + 256],
                             True, True, [kT[n], qT[n]], [sT])
                    eT = eT_r.next()
                    P.act(eT[:], sT[:], AF.Exp, [sT], [eT], scale=128.0 ** -0.5)
                    eTs[i] = eT
                if i - LOOK < 0:
                    continue
                qb, kt = steps[i - LOOK]
                eT = eTs.pop(i - LOOK)
                q0 = qb * 256
                for n in range(2):
                    for j in range(2):
                        P.mm(po[n][j][:, 0:257], eT[:, n * 256 + j * 128:n * 256 + (j + 1) * 128], va[:, kt, 0:257],
                             kt == 0, kt == NT - 1, [eT, vab[kt]], [po[n][j]])
                if kt != NT - 1:
                    continue
                pc = pc_r.next()
                for n2 in range(2):
                    for j in range(2):
                        P.cp("dve" if j == 0 else "act", pc[:, 2 * n2 + j, :], po[n2][j][:, 0:257],
                             [po[n2][j]], [pc])
                for j in range(2):
                    st = st_r.next()
                    P.op("dve", lambda e, st=st, pc=pc, j=j: e.reciprocal(st[:, 0:1], pc[:, j, 256:257]), [pc], [st])
                    P.op("dve", lambda e, st=st, pc=pc, j=j: e.reciprocal(st[:, 1:2], pc[:, 2 + j, 256:257]), [pc], [st])
                    P.tt("dve", st[:, 1:2], st[:, 1:2], lam[:, 2:3], ALU.mult, [st, lam], [st])
                    ot = ot_r.next()
                    P.ts("pool", ot[:], pc[:, 2 + j, 0:256], st[:, 1:2], None, ALU.mult, ALU.bypass, [pc, st], [ot])
                    on = on_r.next()
                    P.stt(on[:], pc[:, j, 0:256], st[:, 0:1], ot[:], ALU.mult, ALU.subtract, [pc, st, ot], [on])
                    P.op("dve", lambda e, on=on, st=st, ot=ot: e.scalar_tensor_tensor(
                        out=ot[:], in0=on[:], scalar=1.0, in1=on[:], op0=ALU.mult, op1=ALU.mult,
                        accum_out=st[:, 2:3]), [on], [ot, st])
                    P.ts("dve", st[:, 3:4], st[:, 2:3], 1.0 / (256.0 * (1.0 - LAM_INIT) ** 2),
                         EPS / (1.0 - LAM_INIT) ** 2, ALU.mult, ALU.add, [st], [st])
                    P.tt("pool", st[:, 4:5], st[:, 3:4], mhalf[:, 0:1], ALU.pow, [st, mhalf], [st])
                    P.stt(on[:], on[:], st[:, 4:5], cogb[:], ALU.mult, ALU.mult, [on, st, cogb], [on])

                    def fin(on=on, tq=q0 + j * 128):
                        ps = pbe.next()
                        for vc in range(2):
                            P.tr(ps[:, vc * 128:(vc + 1) * 128], on[:, vc * 128:(vc + 1) * 128], self.ident,
                                 [on, self.cst], [ps])
                        for vc in range(2):
                            P.tt("dve", ogh[:, vc, tq:tq + 128], ps[:, vc * 128:(vc + 1) * 128],
                                 sgT[vc][:, tq:tq + 128], ALU.mult, [ps, sgT[vc]], [ogh])
                    deferred.append((i + 3 + j, fin))
            for _, fin in deferred:
                fin()
            deferred = []
            for vc in range(2):
                P.dma("sp", d["og"][2 * hd + vc, :, NCTX:L], ogh[:, vc, :], [ogh], [self.db["og"]], f"ogh{vc}")
        P.arena_reset()
        self.out_proj(d["wo1"], d["x1"], d["ctx1"], srcb, d["out"], None, [self.db["out"]], NCTX, True)

    heads_c = range(8)

    def build(self):
        if 0 in self.layers:
            self.layer0()
        if 1 in self.layers:
            self.layer1()
        self.P.fence("sp", self.out_ops)
        self.P.finalize()


def _lay(w, col0, ncol):
    return np.ascontiguousarray(w[:, col0:col0 + ncol].reshape(KC, 128, ncol).transpose(1, 0, 2))


def _consts():
    c = np.zeros((128, 384), np.float32)
    cb = np.zeros((128, 1280), np.float32)
    i = np.arange(128)
    c[:, 0:128] = np.eye(128, dtype=np.float32)
    same = (i[:, None] // 32) == (i[None, :] // 32)
    c[:, 128:256] = (same & (i[:, None] <= i[None, :]))
    c[:, 256:384] = (same & (i[:, None] >= i[None, :]))
    for ch in range(4):
        cb[:, ch * 128:(ch + 1) * 128] = ((i // 32) == ch)[:, None]
    rot = np.zeros((128, 128), np.float32)
    for m in range(128):
        if (m % 64) < 32:
            rot[m + 32, m] = -1.0
        else:
            rot[m - 32, m] = 1.0
    cb[:, 512:640] = rot
    cb[:, 640:768] = 1.0
    same64 = (i[:, None] // 64) == (i[None, :] // 64)
    cb[:, 768:896] = (same64 & (i[:, None] <= i[None, :]))
    cb[:, 896:1024] = (same64 & (i[:, None] >= i[None, :]))
    for ch in range(2):
        cb[:, 1024 + ch * 128:1024 + (ch + 1) * 128] = ((i // 64) == ch)[:, None]
    return c, cb


def _rope_tables():
    half = 32
    inv = (10000.0 ** (-np.arange(half, dtype=np.float32) / half)).astype(np.float32)
    t = np.arange(SEQ)
    pos_r = (t // 64).astype(np.float32)
    pos_c = (t % 64).astype(np.float32)
    dd = np.arange(128)
    pos = np.where((dd // 64)[:, None] == 0, pos_r[None, :], pos_c[None, :]).astype(np.float32)
    ang = (pos * inv[dd % 32][:, None]).astype(np.float32)
    return np.stack([np.cos(ang), np.sin(ang)]).astype(np.float32)


def _shared_inputs(inp, layers):
    m = {}
    wa = inp["w_ada"]
    m["wada"] = np.ascontiguousarray(wa.reshape(2, KC, 128, 48, 128).transpose(0, 3, 2, 1, 4)).reshape(2, 48, 128, KC * 128)
    m["badac"] = np.ascontiguousarray(inp["b_ada"].reshape(2, 48, 128).transpose(0, 2, 1))
    m["ngc"] = np.ascontiguousarray(inp["norm_gain"].reshape(2, KC, 128).transpose(0, 2, 1))
    m["cst"], m["cstb"] = _consts()
    if 0 in layers:
        w = inp["w_in_even"][0]
        m["w0a"] = np.stack([np.stack([_lay(w, base + h * 128, 128) for base in (0, 1024, 2048, 3072, 6176)])
                             for h in range(8)])
        m["w0b"] = np.stack([np.stack([_lay(w, 4096 + g * 128, 128), _lay(w, 4608 + g * 128, 128),
                                       _lay(w, 5120 + g * 256, 128), _lay(w, 5120 + g * 256 + 128, 128),
                                       _lay(w, 7200 + g * 256, 128), _lay(w, 7200 + g * 256 + 128, 128)])
                             for g in range(4)])
        m["w0z"] = np.stack([_lay(w, 6144, 16), _lay(w, 6160, 16)])
        m["wa2"] = np.ascontiguousarray(inp["w_a2"][0].transpose(1, 0, 2))
        m["ba2c"] = np.ascontiguousarray(inp["b_a2"][0].reshape(2, 4, 128).transpose(2, 0, 1).reshape(128, 8))
        m["lbl"] = np.ascontiguousarray(inp["lb_logits"].reshape(3, 8, 128).transpose(2, 0, 1))
        m["g0c"] = np.ascontiguousarray(np.stack([inp["a_out_gain"][0], inp["b_out_gain"][0][:128],
                                                  inp["b_out_gain"][0][128:]], axis=1))
        wo = inp["w_out_even"][0]
        m["wo0"] = np.stack([_lay(wo, n * 128, 128) for n in range(16)])
    if 1 in layers:
        w = inp["w_in_odd"][0]
        m["w1"] = np.stack([np.stack([_lay(w, base + h * 256 + n * 128, 128) for base in (0, 2048, 4096, 6144)
                                      for n in range(2)]) for h in range(8)])
        m["qkg"] = np.ascontiguousarray(np.stack([inp["q_norm_gain"][0], inp["k_norm_gain"][0]], axis=1))
        m["lqb"] = np.ascontiguousarray(np.broadcast_to(inp["lambda_qk"][0][None], (128, 4, 128)))
        m["cogb"] = np.ascontiguousarray(np.broadcast_to(inp["c_out_gain"][0][None], (128, 256)))
        wo = inp["w_out_odd"][0]
        m["wo1"] = np.stack([_lay(wo, n * 128, 128) for n in range(16)])
        m["rope"] = _rope_tables()
    return {k: np.ascontiguousarray(v, dtype=np.float32) for k, v in m.items()}


def _core_inputs(inp, b):
    cc = np.stack([inp["c"][b], inp["c_ctx"]], axis=0)
    return {"cT": np.ascontiguousarray(cc.reshape(2, KC, 128).transpose(2, 1, 0), dtype=np.float32)}


def build_nc(layers=(0, 1), **kw):
    nc = bass.Bass("TRN2", target_bir_lowering=False)
    with ExitStack() as st:
        b = Builder(nc, st, layers)
        for k, v in kw.items():
            setattr(b, k, v)
        b.build()
    return nc


def run_layers(inp, layers, x1=None, ctx1=None, cores=8, **kw):
    shared = _shared_inputs(inp, layers)
    nc = build_nc(layers, **kw)
    in_maps = []
    for b in range(cores):
        m = dict(shared)
        m.update(_core_inputs(inp, b))
        if 0 in layers:
            m["x"] = np.ascontiguousarray(inp["x"][b], dtype=np.float32)
            m["ctx"] = np.ascontiguousarray(inp["ctx"][b], dtype=np.float32)
        else:
            m["x1"] = np.ascontiguousarray(x1[b], dtype=np.float32)
            m["ctx1"] = np.ascontiguousarray(ctx1[b], dtype=np.float32)
        in_maps.append(m)
    res = run_bass_kernel_spmd(nc, in_maps, core_ids=list(range(cores)))
    return res.results


def kernel(**inputs):
    inp = {k: np.asarray(v) for k, v in inputs.items()}
    r = run_layers(inp, (0, 1))
    return np.stack([x["out"] for x in r]).astype(np.float32)
```
